# Optimizing a Trainium2 kernel written in Bass

```python
import jax, jax.numpy as jnp
from jax import lax
import numpy as np

D_MODEL = 2048
BATCH = 4
SEQ = 4096
DEPTH = 1

HEAD_DIM = 64
ATTN_WIDTH = D_MODEL // 2
CONV_WIDTH_CH = D_MODEL - ATTN_WIDTH
N_Q_HEADS = ATTN_WIDTH // HEAD_DIM
N_KV_HEADS = 4
GROUP = N_Q_HEADS // N_KV_HEADS
KV_WIDTH = N_KV_HEADS * HEAD_DIM
N_CONV_GROUPS = CONV_WIDTH_CH // HEAD_DIM
CONV_WIDTH = 3
WINDOW = 128
ROT_DIM = HEAD_DIM // 4
ROPE_THETA = 500000.0
D_FF = ((8 * D_MODEL // 3 + 255) // 256) * 256
IN_WIDTH = ATTN_WIDTH + 2 * KV_WIDTH + 3 * CONV_WIDTH_CH
SPLITS = [ATTN_WIDTH,
          ATTN_WIDTH + KV_WIDTH,
          ATTN_WIDTH + 2 * KV_WIDTH,
          ATTN_WIDTH + 2 * KV_WIDTH + CONV_WIDTH_CH,
          ATTN_WIDTH + 2 * KV_WIDTH + 2 * CONV_WIDTH_CH]
ATTN_SCALE = HEAD_DIM ** -0.5
DEEPNORM_ALPHA = (2 * DEPTH) ** 0.25
DEEPNORM_BETA = (8 * DEPTH) ** -0.25
LN_EPS = 1e-5
RMS_EPS = 1e-6

kernel_name = "hymba_conv_swa_sink_deepnorm_layer"


def _layer_norm(x, g, b):
    xf = x.astype(jnp.float32)
    mu = jnp.mean(xf, axis=-1, keepdims=True)
    var = jnp.mean(jnp.square(xf - mu), axis=-1, keepdims=True)
    y = (xf - mu) * lax.rsqrt(var + LN_EPS)
    return (y * g.astype(jnp.float32) + b.astype(jnp.float32)).astype(x.dtype)


def _rms_norm(x, g):
    xf = x.astype(jnp.float32)
    y = xf * lax.rsqrt(jnp.mean(jnp.square(xf), axis=-1, keepdims=True) + RMS_EPS)
    return (y * g.astype(jnp.float32)).astype(x.dtype)


def _partial_rope(t, positions):
    inv_freq = ROPE_THETA ** (-jnp.arange(0, ROT_DIM, 2, dtype=jnp.float32) / ROT_DIM)
    ang = positions.astype(jnp.float32)[..., None] * inv_freq
    cos = jnp.cos(ang)[:, :, None, :]
    sin = jnp.sin(ang)[:, :, None, :]
    tr = t[..., :ROT_DIM].astype(jnp.float32)
    t1, t2 = tr[..., :ROT_DIM // 2], tr[..., ROT_DIM // 2:]
    rot = jnp.concatenate([t1 * cos - t2 * sin, t2 * cos + t1 * sin], axis=-1).astype(t.dtype)
    return jnp.concatenate([rot, t[..., ROT_DIM:]], axis=-1)


def _sliding_window_attention(q, k, v, sinks):
    b, s = q.shape[0], q.shape[1]
    nb = s // WINDOW
    qb = q.reshape(b, nb, WINDOW, N_KV_HEADS, GROUP, HEAD_DIM)
    kb = k.reshape(b, nb, WINDOW, N_KV_HEADS, HEAD_DIM)
    vb = v.reshape(b, nb, WINDOW, N_KV_HEADS, HEAD_DIM)

    def with_prev(t):
        prev = jnp.concatenate([jnp.zeros_like(t[:, :1]), t[:, :-1]], axis=1)
        return jnp.concatenate([prev, t], axis=2)

    kk, vv = with_prev(kb), with_prev(vb)
    scores = jnp.einsum('bnqhgd,bnkhd->bnhgqk', qb, kk,
                        preferred_element_type=jnp.float32) * ATTN_SCALE
    qi = jnp.arange(WINDOW)[:, None]
    kj = jnp.arange(2 * WINDOW)[None, :]
    rel = qi + WINDOW - kj
    band = (rel >= 0) & (rel < WINDOW)
    first_block = (jnp.arange(nb) == 0)[:, None, None]
    valid = band[None] & ~(first_block & (kj < WINDOW)[None])
    scores = jnp.where(valid[None, :, None, None], scores, -jnp.inf)
    sink_col = jnp.broadcast_to(
        sinks.astype(jnp.float32).reshape(N_KV_HEADS, GROUP)[None, None, :, :, None, None],
        scores.shape[:-1] + (1,))
    probs = jax.nn.softmax(jnp.concatenate([scores, sink_col], axis=-1), axis=-1)[..., :-1]
    out = jnp.einsum('bnhgqk,bnkhd->bnqhgd', probs.astype(v.dtype), vv)
    return out.reshape(b, s, N_Q_HEADS * HEAD_DIM)


def _short_gated_conv(c_gate, b_gate, u, conv_w):
    z = c_gate * u
    s = z.shape[1]
    zp = jnp.pad(z, ((0, 0), (CONV_WIDTH - 1, 0), (0, 0)))
    y = conv_w[0] * zp[:, 0:s] + conv_w[1] * zp[:, 1:s + 1] + conv_w[2] * zp[:, 2:s + 2]
    return b_gate * y


def setup_inputs(seed: int = 0) -> dict:
    key = jax.random.key(seed)
    ks = jax.random.split(key, 16)
    f32 = jnp.float32
    x = jax.random.normal(ks[0], (BATCH, SEQ, D_MODEL), f32)
    offset = jax.random.randint(ks[1], (BATCH, 1), 0, 1024, dtype=jnp.int32)
    positions = (offset + jnp.arange(SEQ, dtype=jnp.int32)[None, :]).astype(jnp.int32)
    w_in = jax.random.normal(ks[2], (DEPTH, D_MODEL, IN_WIDTH), f32) * D_MODEL ** -0.5
    conv_w = jax.random.normal(ks[3], (DEPTH, CONV_WIDTH, CONV_WIDTH_CH), f32) * CONV_WIDTH ** -0.5
    sinks = jax.random.normal(ks[4], (DEPTH, N_Q_HEADS), f32) * 0.5
    g_attn = 1.0 + 0.02 * jax.random.normal(ks[5], (DEPTH, ATTN_WIDTH), f32)
    g_conv = 1.0 + 0.02 * jax.random.normal(ks[6], (DEPTH, CONV_WIDTH_CH), f32)
    w_out = jax.random.normal(ks[7], (DEPTH, D_MODEL, D_MODEL), f32) * (DEEPNORM_BETA * D_MODEL ** -0.5)
    ln1_g = 1.0 + 0.02 * jax.random.normal(ks[8], (DEPTH, D_MODEL), f32)
    ln1_b = 0.02 * jax.random.normal(ks[9], (DEPTH, D_MODEL), f32)
    w_gate = jax.random.normal(ks[10], (DEPTH, D_MODEL, D_FF), f32) * D_MODEL ** -0.5
    w_up = jax.random.normal(ks[11], (DEPTH, D_MODEL, D_FF), f32) * D_MODEL ** -0.5
    w_down = jax.random.normal(ks[12], (DEPTH, D_FF, D_MODEL), f32) * (DEEPNORM_BETA * D_FF ** -0.5)
    ln2_g = 1.0 + 0.02 * jax.random.normal(ks[13], (DEPTH, D_MODEL), f32)
    ln2_b = 0.02 * jax.random.normal(ks[14], (DEPTH, D_MODEL), f32)
    return {"x": x, "positions": positions, "w_in": w_in, "conv_w": conv_w,
            "sinks": sinks, "g_attn": g_attn, "g_conv": g_conv, "w_out": w_out,
            "ln1_g": ln1_g, "ln1_b": ln1_b, "w_gate": w_gate, "w_up": w_up,
            "w_down": w_down, "ln2_g": ln2_g, "ln2_b": ln2_b}


def reference(x, positions, w_in, conv_w, sinks, g_attn, g_conv, w_out,
              ln1_g, ln1_b, w_gate, w_up, w_down, ln2_g, ln2_b):
    b, s, _ = x.shape
    h = x
    for l in range(DEPTH):
        proj = h @ w_in[l]
        q, k, v, c_gate, b_gate, u = jnp.split(proj, SPLITS, axis=-1)
        q = _partial_rope(q.reshape(b, s, N_Q_HEADS, HEAD_DIM), positions)
        k = _partial_rope(k.reshape(b, s, N_KV_HEADS, HEAD_DIM), positions)
        v = v.reshape(b, s, N_KV_HEADS, HEAD_DIM)
        attn = _sliding_window_attention(q, k, v, sinks[l])
        conv = _short_gated_conv(c_gate, b_gate, u, conv_w[l])
        mixed = jnp.concatenate([_rms_norm(attn, g_attn[l]), _rms_norm(conv, g_conv[l])], axis=-1)
        mix_out = mixed @ w_out[l]
        h = _layer_norm(DEEPNORM_ALPHA * h + mix_out, ln1_g[l], ln1_b[l])
        ffn = (jax.nn.silu(h @ w_gate[l]) * (h @ w_up[l])) @ w_down[l]
        h = _layer_norm(DEEPNORM_ALPHA * h + ffn, ln2_g[l], ln2_b[l])
    return h
```

```python
import numpy as np
import ml_dtypes
import concourse.bass as bass
import concourse.mybir as mybir
from concourse.bass_utils import run_bass_kernel_spmd

F32 = mybir.dt.float32
BF16 = mybir.dt.bfloat16
I32 = mybir.dt.int32
U8 = mybir.dt.uint8
ALU = mybir.AluOpType
AF = mybir.ActivationFunctionType

D = 2048
DFF = 5632
INW = 4608
NCH = 16
T = 512
NPASS = 4
NTT = 4
NTILE = 17
TOK = NTILE * 128
ALPHA = 2.0 ** 0.25
ATTN_SCALE = 0.125
LN_EPS = 1e-5
RMS_EPS = 1e-6
TWO_PI = 6.283185307179586
C1 = 6.28125
C2 = TWO_PI - C1
NWS = 3


def _dsize(dt):
    return {F32: 4, BF16: 2, I32: 4, U8: 1}[dt]


class Op:
    __slots__ = ("eng", "fn", "r", "w", "dma", "inc", "seq", "barrier")

    def __init__(self, eng, fn, r, w, dma):
        self.eng, self.fn, self.r, self.w, self.dma = eng, fn, r, w, dma
        self.inc = False
        self.seq = 0
        self.barrier = False


class Prog:
    PART = ("pe", "act", "dve", "sp")

    def __init__(self, nc):
        self.nc = nc
        self.ops = []
        self.E = {"pe": nc.tensor, "act": nc.scalar, "dve": nc.vector, "pool": nc.gpsimd, "sp": nc.sync}

    def add(self, eng, fn, r=(), w=(), dma=None):
        r, w = tuple(r), tuple(w)
        w = w + tuple(k for k in r if isinstance(k, tuple) and k[0] == "ps" and k not in w)
        self.ops.append(Op(eng, fn, r, w, dma))

    def barrier(self, exempt=()):
        o = Op("barrier", None, (), (), None)
        o.barrier = True
        o.r = tuple(exempt)
        self.ops.append(o)

    def emit(self, final_wait_eng="sp"):
        nc, ops = self.nc, self.ops
        n = len(ops)
        last_w = {}
        rd_eng = {}
        rd_dma = {}
        deps = [None] * n
        last_on_eng = {}
        last_dma_key = {}
        pending = {}
        for i, op in enumerate(ops):
            if op.barrier:
                fr = set(last_on_eng.values()) | set(last_dma_key.values())
                for e in self.PART:
                    if e not in op.r:
                        pending[e] = pending.get(e, set()) | fr
                for dct in (last_w, rd_eng, rd_dma):
                    for k in [k for k in dct if isinstance(k, tuple) and k and k[0] == "A"]:
                        del dct[k]
                continue
            d = {}
            for k in op.r:
                j = last_w.get(k)
                if j is not None:
                    d[j] = True
            for k in op.w:
                j = last_w.get(k)
                if j is not None:
                    d.setdefault(j, False)
                for j in rd_eng.get(k, {}).values():
                    d.setdefault(j, False)
                for j in rd_dma.get(k, ()):
                    d.setdefault(j, False)
            if op.eng in pending:
                for j in pending.pop(op.eng):
                    d[j] = True
            d.pop(i, None)
            deps[i] = d
            for k in op.r:
                if op.dma is not None:
                    rd_dma.setdefault(k, []).append(i)
                else:
                    rd_eng.setdefault(k, {})[op.eng] = i
            for k in op.w:
                last_w[k] = i
                rd_eng[k] = {}
                rd_dma[k] = []
            if op.dma is not None:
                if op.eng != "pool":
                    last_dma_key[op.dma] = i
            elif op.eng in self.PART:
                last_on_eng[op.eng] = i
        need = [None] * n
        for i, op in enumerate(ops):
            if op.barrier:
                continue
            lst = []
            for j, raw in deps[i].items():
                pj = ops[j]
                if pj.dma is not None:
                    lst.append(j)
                elif pj.eng == op.eng:
                    if op.dma is not None:
                        lst.append(j)
                        pj.inc = True
                    elif op.eng == "pe":
                        continue
                    else:
                        lst.append(j)
                        pj.inc = True
                else:
                    lst.append(j)
                    pj.inc = True
            need[i] = lst
        self.need_dbg = need
        cnt = {}
        for op in ops:
            if op.barrier:
                continue
            if op.dma is not None:
                key = ("dma", op.dma)
                cnt[key] = cnt.get(key, 0) + 16
                op.seq = cnt[key]
            elif op.inc:
                cnt[op.eng] = cnt.get(op.eng, 0) + 1
                op.seq = cnt[op.eng]
        for k, v in cnt.items():
            assert v < 32000, (k, v)
        sems = {}

        def sem_of(op):
            key = ("dma", op.dma) if op.dma is not None else op.eng
            if key not in sems:
                sems[key] = nc.alloc_semaphore(f"s{len(sems)}")
            return sems[key]

        waited = {e: {} for e in self.E}
        nwait = 0
        for i, op in enumerate(ops):
            if op.barrier:
                continue
            e = self.E[op.eng]
            wl = {}
            for j in need[i]:
                pj = ops[j]
                s = sem_of(pj)
                if pj.seq > wl.get(s, (0, None))[0]:
                    wl[s] = (pj.seq, s)
            for s, (v, _) in wl.items():
                if waited[op.eng].get(s, 0) < v:
                    e.wait_ge(s, v)
                    waited[op.eng][s] = v
                    nwait += 1
            ins = op.fn(e)
            if op.dma is not None:
                ins.then_inc(sem_of(op), 16)
            elif op.inc:
                ins.then_inc(sem_of(op), 1)
        fe = self.E[final_wait_eng]
        for key, v in cnt.items():
            if isinstance(key, tuple) and key[0] == "dma":
                fe.wait_ge(sems[key], v)
        self.stats = dict(n_ops=n, n_wait=nwait, n_sems=len(sems), cnt={str(k): v for k, v in cnt.items()})


def build(debug=False, stop_after=None, npass=NPASS):
    nc = bass.Bass("TRN2", target_bir_lowering=False)
    pr = Prog(nc)

    def din(name, shape, dt):
        return nc.dram_tensor(name, list(shape), dt, kind="ExternalInput")

    x_c = din("x_c", [TOK, D], F32)
    pos_c = din("pos_c", [128, NTILE], I32)
    w_in = din("w_in", [D, INW], F32)
    w_out = din("w_out", [D, D], F32)
    w_gate = din("w_gate", [D, DFF], F32)
    w_up = din("w_up", [D, DFF], F32)
    w_down = din("w_down", [DFF, D], F32)
    convw_d = din("convw", [128, 24], F32)
    gconv_d = din("gconv", [128, 8], F32)
    gattn_d = din("gattn", [1, 1024], F32)
    sinks_d = din("sinks", [1, 16], F32)
    ln1g_d = din("ln1g", [1, D], F32)
    ln1b_d = din("ln1b", [1, D], F32)
    ln2g_d = din("ln2g", [1, D], F32)
    ln2b_d = din("ln2b", [1, D], F32)
    ident_d = din("ident", [128, 128], BF16)
    maskp_d = din("maskp", [128, 128], BF16)
    maskc_d = din("maskc", [128, 128], BF16)
    maskp0_d = din("maskp0", [128, 128], BF16)
    invf_d = din("invf", [128, 8], F32)
    out_c = nc.dram_tensor("out_c", [2048, D], F32, kind="ExternalOutput")

    total = int(nc.sbuf_bytes_remaining) - 64
    slab = nc.alloc_sbuf_tensor("slab", [128, total], U8)
    base = int(nc.lookup_mloc(slab).addr)
    state = {"off": base, "limit": base + total}

    def sb(name, free, dt):
        nb_ = (free * _dsize(dt) + 31) // 32 * 32
        h = nc.alloc_sbuf_tensor_at(name, [128, free], dt, offset=state["off"])
        state["off"] += nb_
        assert state["off"] <= state["limit"], (name, state["off"] - base, total)
        return h

    ident = sb("ident", 128, BF16)
    ones = sb("ones", 128, BF16)
    maskp = sb("maskp", 128, BF16)
    maskc = sb("maskc", 128, BF16)
    maskp0 = sb("maskp0", 128, BF16)
    convw = sb("convw", 24, F32)
    gconv = sb("gconv", 8, F32)
    gattn = sb("gattn", 1024, F32)
    expsink = sb("expsink", 16, F32)
    cs_cos = sb("cs_cos", NTILE * 8, F32)
    cs_sin = sb("cs_sin", NTILE * 8, F32)
    cs_nsin = sb("cs_nsin", NTILE * 8, F32)
    KT = sb("KT", 2 * TOK, BF16)
    V = sb("V", NTILE * 260, BF16)
    zh = sb("zh", 16, F32)
    wsl = [sb(f"w{i}", 8192, BF16) for i in range(NWS)]
    xb = [sb(f"xb{i}", 2048, BF16) for i in range(4)]
    mT = sb("mT", 16 * 512, BF16)
    small = sb("small", 384, F32)
    fence = sb("fence", 8, F32)
    arena = state["off"]
    G0 = 51200
    xT = sb("xT", 16 * 640, BF16)
    qtok = sb("qtok", 5 * 1024, BF16)
    ktok = sb("ktok", 5 * 256, BF16)
    qT = sb("qT", 8 * 512, BF16)
    ropet = [sb(f"ropet{i}", 256, F32) for i in range(2)]
    ab0_end = state["off"]
    state["off"] = arena + G0
    usb = [sb(f"usb{i}", 512, F32) for i in range(2)]
    zc = [sb(f"zc{i}", 514, F32) for i in range(2)]
    ybuf = [sb(f"y{i}", 512, F32) for i in range(2)]
    obuf = sb("obuf", 8 * 512, F32)
    sq = [sb(f"sq{i}", 512, BF16) for i in range(2)]
    rstd_c = sb("rstd_c", 512, F32)
    pT = [sb(f"pT{i}", 1024, BF16) for i in range(3)]
    atok = [sb(f"atok{i}", 1024, F32) for i in range(2)]
    mtok = [sb(f"mtok{i}", 1024, BF16) for i in range(2)]
    junk = sb("junk", 512, BF16)
    ab_end = state["off"]
    state["off"] = arena
    h1b = [sb(f"h1b{i}", 2048, BF16) for i in range(2)]
    h1T = sb("h1T", 16 * 512, BF16)
    actT = sb("actT", 22 * 512, BF16)
    sg = [sb(f"sg{i}", 512, F32) for i in range(2)]
    cd0_end = state["off"]
    assert ab0_end <= arena + G0 and cd0_end <= arena + G0, (ab0_end - arena, cd0_end - arena, G0)
    state["off"] = arena + G0
    lngb = sb("lngb", 2 * 2048, F32)
    h1 = sb("h1", 4 * 2048, F32)
    junk2 = sb("junk2", 2048, BF16)
    cd_end = state["off"]
    state["off"] = max(ab_end, cd_end)

    ps = nc.alloc_psum_tensor("ps", [128, 8 * 512], F32)
    psb = ps.bitcast(BF16)

    def AP(t, off, dims, p0=0, npart=128):
        Fr = t.shape[1]
        return bass.AP(t, p0 * Fr + off, [[Fr, npart]] + [list(x) for x in dims])

    def DAP(t, off, dims):
        return bass.AP(t, off, [list(x) for x in dims])

    def PS(b, off=0, n=512):
        return AP(ps, b * 512 + off, [[1, n]])

    bank_state = {"i": 0}

    def nb(allowed=range(8)):
        allowed = list(allowed)
        while True:
            b = bank_state["i"] % 8
            bank_state["i"] += 1
            if b in allowed:
                return b

    sm_next = {"i": 0}

    def smalloc(n=1):
        o = sm_next["i"]
        sm_next["i"] += n
        assert sm_next["i"] <= 384
        return o

    CK = "const"
    dumps = {}
    if debug:
        dumps = {"KT": (KT, BF16), "V": (V, BF16), "mT": (mT, BF16), "cs_cos": (cs_cos, F32), "cs_sin": (cs_sin, F32)}
        if stop_after in ("A1", "A2", "A3", "A4", "B"):
            dumps.update({"xT": (xT, BF16), "qT": (qT, BF16), "qtok": (qtok, BF16), "ktok": (ktok, BF16), "obuf": (obuf, F32)})
        else:
            dumps.update({"h1": (h1, F32), "h1T": (h1T, BF16), "small": (small, F32)})
        for name, (t, dt) in dumps.items():
            if name.startswith("cs_"):
                continue
            pr.add("dve", (lambda e, t=t: e.memset(AP(t, 0, [[1, t.shape[1]]]), 0.0)))
        pr.barrier()
    cname = {}
    for (nm, dst, src, nfree) in (("ident", ident, ident_d, 128), ("maskp", maskp, maskp_d, 128), ("maskc", maskc, maskc_d, 128),
                                  ("maskp0", maskp0, maskp0_d, 128), ("convw", convw, convw_d, 24), ("gconv", gconv, gconv_d, 8)):
        cname[id(dst)] = nm
        pr.add("sp", (lambda e, dst=dst, src=src, nfree=nfree: e.dma_start(out=AP(dst, 0, [[1, nfree]]), in_=src.ap())),
               w=[(CK, nm)], dma=("c", nm))
    pr.add("sp", lambda e: e.dma_start(out=AP(gattn, 0, [[1, 1024]]), in_=DAP(gattn_d, 0, [[0, 128], [1, 1024]])),
           w=[(CK, "gattn")], dma=("c", "gattn"))
    pr.add("sp", lambda e: e.dma_start(out=AP(expsink, 0, [[1, 16]]), in_=DAP(sinks_d, 0, [[0, 128], [1, 16]])),
           w=[(CK, "sink")], dma=("c", "sink"))
    pr.add("act", lambda e: e.activation(out=AP(expsink, 0, [[1, 16]]), in_=AP(expsink, 0, [[1, 16]]), func=AF.Exp),
           r=[(CK, "sink")], w=[(CK, "sink")])
    pr.add("dve", lambda e: e.memset(AP(ones, 0, [[1, 128]]), 1.0), w=[(CK, "ones")])
    pr.add("dve", lambda e: e.memset(AP(V, 64, [[65, NTILE * 4]]), 1.0), w=[("V", t) for t in range(NTILE)])
    o_eps_rms = smalloc()
    o_eps_ln = smalloc()
    pr.add("dve", lambda e: e.memset(AP(small, o_eps_rms, [[1, 1]]), RMS_EPS), w=[(CK, "eps1")])
    pr.add("dve", lambda e: e.memset(AP(small, o_eps_ln, [[1, 1]]), LN_EPS), w=[(CK, "eps2")])
    EPS_RMS = AP(small, o_eps_rms, [[1, 1]])
    EPS_LN = AP(small, o_eps_ln, [[1, 1]])

    NA = NTILE * 8
    _save_off = state["off"]
    state["off"] = arena + G0
    posi = sb("posi", NTILE, I32)
    posf = sb("posf", NTILE, F32)
    invf = sb("invf", 8, F32)
    ang = sb("ang", NA, F32)
    kf = sb("kf", NA, F32)
    ki = sb("ki", NA, I32)
    rr = sb("rr", NA, F32)
    rc = sb("rc", NA, F32)
    mm_ = sb("mm_", NA, F32)
    state["off"] = _save_off
    pr.add("sp", lambda e: e.dma_start(out=AP(posi, 0, [[1, NTILE]]), in_=pos_c.ap()), w=["posi"], dma=("c", "posi"))
    pr.add("sp", lambda e: e.dma_start(out=AP(invf, 0, [[1, 8]]), in_=invf_d.ap()), w=["invf"], dma=("c", "invf"))
    pr.add("dve", lambda e: e.tensor_copy(out=AP(posf, 0, [[1, NTILE]]), in_=AP(posi, 0, [[1, NTILE]])), r=["posi"], w=["posf"])
    pr.add("dve", lambda e: e.tensor_tensor(out=AP(ang, 0, [[8, NTILE], [1, 8]]), in0=AP(posf, 0, [[1, NTILE], [0, 8]]),
                                            in1=AP(invf, 0, [[0, NTILE], [1, 8]]), op=ALU.mult), r=["posf", "invf"], w=["ang"])
    A1 = lambda t: AP(t, 0, [[1, NA]])
    pr.add("dve", lambda e: e.tensor_scalar(out=A1(kf), in0=A1(ang), scalar1=1.0 / TWO_PI, scalar2=None, op0=ALU.mult), r=["ang"], w=["kf"])
    pr.add("dve", lambda e: e.tensor_copy(out=A1(ki), in_=A1(kf)), r=["kf"], w=["ki"])
    pr.add("dve", lambda e: e.tensor_copy(out=A1(kf), in_=A1(ki)), r=["ki"], w=["kf"])
    pr.add("dve", lambda e: e.scalar_tensor_tensor(out=A1(rr), in0=A1(kf), scalar=-C1, in1=A1(ang), op0=ALU.mult, op1=ALU.add), r=["kf", "ang"], w=["rr"])
    pr.add("dve", lambda e: e.scalar_tensor_tensor(out=A1(rr), in0=A1(kf), scalar=-C2, in1=A1(rr), op0=ALU.mult, op1=ALU.add), r=["kf", "rr"], w=["rr"])

    def wrap(t, key):
        pr.add("dve", lambda e: e.tensor_scalar(out=A1(mm_), in0=A1(t), scalar1=np.pi, scalar2=-TWO_PI, op0=ALU.is_gt, op1=ALU.mult), r=[key], w=["mm_"])
        pr.add("dve", lambda e: e.tensor_tensor(out=A1(t), in0=A1(t), in1=A1(mm_), op=ALU.add), r=[key, "mm_"], w=[key])
        pr.add("dve", lambda e: e.tensor_scalar(out=A1(mm_), in0=A1(t), scalar1=-np.pi, scalar2=TWO_PI, op0=ALU.is_lt, op1=ALU.mult), r=[key], w=["mm_"])
        pr.add("dve", lambda e: e.tensor_tensor(out=A1(t), in0=A1(t), in1=A1(mm_), op=ALU.add), r=[key, "mm_"], w=[key])
        pr.add("dve", lambda e: e.tensor_scalar(out=A1(t), in0=A1(t), scalar1=3.1415925, scalar2=-3.1415925, op0=ALU.min, op1=ALU.max), r=[key], w=[key])

    wrap(rr, "rr")
    pr.add("dve", lambda e: e.tensor_scalar(out=A1(rc), in0=A1(rr), scalar1=float(np.pi / 2), scalar2=None, op0=ALU.add), r=["rr"], w=["rc"])
    wrap(rc, "rc")
    pr.add("act", lambda e: e.activation(out=A1(cs_sin), in_=A1(rr), func=AF.Sin), r=["rr"], w=["cs"])
    pr.add("act", lambda e: e.activation(out=A1(cs_cos), in_=A1(rc), func=AF.Sin), r=["rc"], w=["cs"])
    pr.add("act", lambda e: e.mul(out=A1(cs_nsin), in_=A1(cs_sin), mul=-1.0), r=["cs"], w=["cs"])

    ws_i = {"i": 0}
    WK = lambda s_: [("w", s_, 0), ("w", s_, 1), ("w", s_, 2)]

    def next_ws():
        s_ = ws_i["i"] % NWS
        ws_i["i"] += 1
        return s_

    sm_base = sm_next["i"]

    import os as _os
    pending_tail = None
    for p in range(npass if stop_after != "K" else 0):
        tiles = ([0] if p == 0 else []) + [1 + 4 * p + i for i in range(4)]
        own = tiles[-4:]
        sm_next["i"] = sm_base

        def xcol(tile):
            return 0 if tile == 0 else 128 + 128 * (tile - (1 + 4 * p))

        def xTkeys(tile):
            return [("A", "xT", tile, 0), ("A", "xT", tile, 1)]

        def capture(fn):
            old = pr.ops
            pr.ops = []
            fn()
            lst = pr.ops
            pr.ops = old
            return lst

        abank_state = {"r": range(8) if p == 0 else range(4, 8)}
        a_mark = {"i": None}

        def block_a():
            wq_slots = []

            def wq_loads(cgs):
                for cg in cgs:
                    s = next_ws()
                    wq_slots.append(s)
                    pr.add("pool", (lambda e, cg=cg, s=s: e.dma_start(out=AP(wsl[s], 0, [[512, 16], [1, 512]]),
                                                                      in_=DAP(w_in, cg * 512, [[INW, 128], [128 * INW, 16], [1, 512]]))),
                           w=WK(s), dma=("w", s, 0))
            for ti, tile in enumerate(tiles):
                s = (tile) % 4
                pr.add("pool", (lambda e, tile=tile, s=s: e.dma_start(out=AP(xb[s], 0, [[1, 2048]]), in_=x_c[tile * 128:(tile + 1) * 128, :])),
                       w=[("xb", s)], dma=("xb", s))
                if ti == 1:
                    wq_loads([0])
                if ti == len(tiles) - 1:
                    wq_loads([1, 2])
                b0, b1 = nb(abank_state["r"]), nb(abank_state["r"])
                for c in range(16):
                    b = b0 if c < 8 else b1
                    pr.add("pe", (lambda e, c=c, b=b, s=s: e.transpose(out=AP(psb, b * 1024 + (c % 8) * 128, [[1, 128]]),
                                                                      in_=AP(xb[s], c * 128, [[1, 128]]), identity=AP(ident, 0, [[1, 128]]))),
                           r=[("xb", s), (CK, "ident")], w=[("ps", b)])
                col = xcol(tile)
                pr.add("act", (lambda e, b0=b0, col=col: e.copy(out=AP(xT, col, [[640, 8], [1, 128]]), in_=AP(psb, b0 * 1024, [[128, 8], [1, 128]]))),
                       r=[("ps", b0)], w=[("A", "xT", tile, 0)])
                pr.add("dve", (lambda e, b1=b1, col=col: e.tensor_copy(out=AP(xT, 8 * 640 + col, [[640, 8], [1, 128]]), in_=AP(psb, b1 * 1024, [[128, 8], [1, 128]]))),
                       r=[("ps", b1)], w=[("A", "xT", tile, 1)])

            for cg in range(3):
                if cg == 2:
                    abank_state["r"] = range(8)
                    a_mark["i"] = len(pr.ops)
                s = wq_slots[cg]
                for tile in (tiles if cg == 2 else own):
                    b = nb(abank_state["r"])
                    col = xcol(tile)
                    for c in range(16):
                        pr.add("pe", (lambda e, c=c, b=b, s=s, col=col: e.matmul(PS(b), lhsT=AP(xT, c * 640 + col, [[1, 128]]), rhs=AP(wsl[s], c * 512, [[1, 512]]),
                                                                                start=(c == 0), stop=(c == 15))),
                               r=xTkeys(tile) + WK(s), w=[("ps", b)])
                    tl = tile - (1 + 4 * p) + 1
                    if cg < 2:
                        qo = tl * 1024 + cg * 64
                        pr.add("act", (lambda e, b=b, qo=qo: e.copy(out=AP(qtok, qo + 16, [[128, 8], [1, 48]]), in_=AP(ps, b * 512 + 16, [[64, 8], [1, 48]]))),
                               r=[("ps", b)], w=[("A", "qtok", tl, cg, "p")])
                        rope_src = lambda off, b=b: AP(ps, b * 512 + off, [[64, 8], [1, 8]])
                        rope_dst = lambda off, qo=qo: AP(qtok, qo + off, [[128, 8], [1, 8]])
                        nh = 8
                        rkey = ("A", "qtok", tl, cg, "r")
                    else:
                        ko = tl * 256
                        pr.add("act", (lambda e, b=b, ko=ko: e.copy(out=AP(ktok, ko + 16, [[64, 2], [128, 2], [1, 48]]), in_=AP(ps, b * 512 + 16, [[128, 2], [64, 2], [1, 48]]))),
                               r=[("ps", b)], w=[("A", "ktok", tl, "p")])
                        pr.add("act", (lambda e, b=b, tile=tile: e.copy(out=AP(V, tile * 260, [[65, 4], [1, 64]]), in_=AP(ps, b * 512 + 256, [[64, 4], [1, 64]]))),
                               r=[("ps", b)], w=[("V", tile)])
                        rope_src = lambda off, b=b: AP(ps, b * 512 + off, [[128, 2], [64, 2], [1, 8]])
                        rope_dst = lambda off, ko=ko: AP(ktok, ko + off, [[64, 2], [128, 2], [1, 8]])
                        nh = 4
                        rkey = ("A", "ktok", tl, "r")
                    rs = (tile + cg) % 2
                    if nh == 8:
                        bc = lambda t_, tile=tile: AP(t_, tile * 8, [[0, 8], [1, 8]])
                        tmp = lambda off, rs=rs: AP(ropet[rs], off, [[16, 8], [1, 8]])
                    else:
                        bc = lambda t_, tile=tile: AP(t_, tile * 8, [[0, 2], [0, 2], [1, 8]])
                        tmp = lambda off, rs=rs: AP(ropet[rs], off, [[32, 2], [16, 2], [1, 8]])
                    tk = ("A", "ropet", rs)
                    pr.add("dve", (lambda e, rope_src=rope_src, bc=bc, tmp=tmp: e.tensor_tensor(out=tmp(0), in0=rope_src(8), in1=bc(cs_nsin), op=ALU.mult)),
                           r=[("ps", b), "cs"], w=[tk])
                    pr.add("dve", (lambda e, rope_src=rope_src, bc=bc, tmp=tmp: e.tensor_tensor(out=tmp(8), in0=rope_src(0), in1=bc(cs_sin), op=ALU.mult)),
                           r=[("ps", b), "cs"], w=[tk])
                    pr.add("dve", (lambda e, rope_src=rope_src, bc=bc, tmp=tmp: e.tensor_tensor(out=tmp(128), in0=rope_src(0), in1=bc(cs_cos), op=ALU.mult)),
                           r=[("ps", b), "cs"], w=[tk])
                    pr.add("dve", (lambda e, rope_src=rope_src, bc=bc, tmp=tmp: e.tensor_tensor(out=tmp(136), in0=rope_src(8), in1=bc(cs_cos), op=ALU.mult)),
                           r=[("ps", b), "cs"], w=[tk])
                    pr.add("dve", (lambda e, rope_dst=rope_dst, tmp=tmp: e.tensor_tensor(out=rope_dst(0), in0=tmp(0), in1=tmp(128), op=ALU.add)),
                           r=[tk], w=[rkey])
                    pr.add("dve", (lambda e, rope_dst=rope_dst, tmp=tmp: e.tensor_tensor(out=rope_dst(8), in0=tmp(8), in1=tmp(136), op=ALU.add)),
                           r=[tk], w=[rkey])

            for tile in tiles:
                tl = tile - (1 + 4 * p) + 1
                if tile != 0:
                    b = nb(abank_state["r"])
                    for j in range(8):
                        pr.add("pe", (lambda e, j=j, b=b, tl=tl: e.transpose(out=AP(psb, b * 1024 + j * 128, [[1, 128]]), in_=AP(qtok, tl * 1024 + j * 128, [[1, 128]]),
                                                                            identity=AP(ident, 0, [[1, 128]]))),
                               r=[("A", "qtok", tl, 0, "p"), ("A", "qtok", tl, 0, "r"), ("A", "qtok", tl, 1, "p"), ("A", "qtok", tl, 1, "r"), (CK, "ident")], w=[("ps", b)])
                    tt = tl - 1
                    pr.add("act", (lambda e, b=b, tt=tt: e.copy(out=AP(qT, tt * 128, [[512, 8], [1, 128]]), in_=AP(psb, b * 1024, [[128, 8], [1, 128]]))),
                           r=[("ps", b)], w=[("A", "qT", tt)])
                b = nb(abank_state["r"])
                for c in range(2):
                    pr.add("pe", (lambda e, c=c, b=b, tl=tl: e.transpose(out=AP(psb, b * 1024 + c * 128, [[1, 128]]), in_=AP(ktok, tl * 256 + c * 128, [[1, 128]]),
                                                                        identity=AP(ident, 0, [[1, 128]]))),
                           r=[("A", "ktok", tl, "p"), ("A", "ktok", tl, "r"), (CK, "ident")], w=[("ps", b)])
                pr.add("dve", (lambda e, b=b, tile=tile: e.tensor_copy(out=AP(KT, tile * 128, [[TOK, 2], [1, 128]]), in_=AP(psb, b * 1024, [[128, 2], [1, 128]]))),
                       r=[("ps", b)], w=[("KT", tile)])


        a_ops = capture(block_a)
        if pending_tail is None:
            pr.ops.extend(a_ops)
        else:
            g0keys = [("h1b", 0), ("h1b", 1), ("sg", 0), ("sg", 1)] + [("h1T", t_, h_) for t_ in range(4) for h_ in range(2)] + [("actT", f_) for f_ in range(22)]
            pr.add("act", lambda e: e.memzero(AP(fence, 0, [[1, 1]])), w=g0keys + ["fence_a"])
            pr.add("dve", lambda e: e.memset(AP(fence, 2, [[1, 1]]), 0.0), w=g0keys + ["fence_d"])
            n_ad = sum(1 for o_ in a_ops if o_.eng in ("act", "dve"))
            per = max(1, (n_ad // 2) // max(1, len(pending_tail)))
            ti_, k_ = 0, 0
            for ai_, o_ in enumerate(a_ops):
                if ai_ == a_mark["i"]:
                    pr.ops.extend(pending_tail[ti_:])
                    ti_ = len(pending_tail)
                pr.ops.append(o_)
                if o_.eng in ("act", "dve"):
                    k_ += 1
                    if k_ % per == 0 and ti_ < len(pending_tail):
                        pr.ops.append(pending_tail[ti_])
                        ti_ += 1
            pr.ops.extend(pending_tail[ti_:])
            pending_tail = None
        pr.barrier()

        BS = 7
        OB = (5, 6)
        rot = range(5)

        conv_banks = {}

        def cbu_mm(j):
            s = next_ws()
            for g in range(3):
                pr.add("pool", (lambda e, j=j, s=s, g=g: e.dma_start(out=AP(wsl[s], g * 128, [[384, 16], [1, 128]]),
                                                                    in_=DAP(w_in, 1536 + 1024 * g + 128 * j, [[INW, 128], [128 * INW, 16], [1, 128]]))),
                       w=[("w", s, g)], dma=("w", s, g))
            bc_, bb_, bu_ = nb(rot), nb(rot), nb(rot)
            bh = None
            for g, b in ((0, bc_), (2, bu_), (1, bb_)):
                for c in range(16):
                    pr.add("pe", (lambda e, c=c, b=b, s=s, g=g: e.matmul(PS(b), lhsT=AP(wsl[s], c * 384 + g * 128, [[1, 128]]), rhs=AP(xT, c * 640 + 128, [[1, 512]]),
                                                                        start=(c == 0), stop=(c == 15))),
                           r=[k for t_ in own for k in xTkeys(t_)] + [("w", s, g)], w=[("ps", b)])
            if p == 0:
                bh = nb(rot)
                for g, off in ((0, 0), (2, 2)):
                    for c in range(16):
                        pr.add("pe", (lambda e, c=c, bh=bh, s=s, g=g, off=off: e.matmul(PS(bh, off, 2), lhsT=AP(wsl[s], c * 384 + g * 128, [[1, 128]]), rhs=AP(xT, c * 640 + 126, [[1, 2]]),
                                                                                      start=(c == 0), stop=(c == 15))),
                               r=xTkeys(0) + [("w", s, g)], w=[("ps", bh)])
            conv_banks[j] = (bc_, bb_, bu_, bh)

        def conv_chain(j):
            bc_, bb_, bu_, bh = conv_banks[j]
            zs = j % 2
            zk = ("A", "zc", zs)
            if p == 0:
                o_uh = smalloc(2)
                pr.add("act", (lambda e, bh=bh, o_uh=o_uh: e.copy(out=AP(small, o_uh, [[1, 2]]), in_=PS(bh, 2, 2))), r=[("ps", bh)], w=[("uh", j)])
                pr.add("dve", (lambda e, bh=bh, o_uh=o_uh, zs=zs: e.tensor_tensor(out=AP(zc[zs], 0, [[1, 2]]), in0=PS(bh, 0, 2), in1=AP(small, o_uh, [[1, 2]]), op=ALU.mult)),
                       r=[("ps", bh), ("uh", j)], w=[zk])
            else:
                pr.add("dve", (lambda e, j=j, zs=zs: e.tensor_copy(out=AP(zc[zs], 0, [[1, 2]]), in_=AP(zh, 2 * j, [[1, 2]]))), r=[("zh", j)], w=[zk])
            pr.add("act", (lambda e, bu_=bu_, zs=zs: e.copy(out=AP(usb[zs], 0, [[1, 512]]), in_=PS(bu_))), r=[("ps", bu_)], w=[("A", "usb", zs)])
            pr.add("dve", (lambda e, bc_=bc_, zs=zs: e.tensor_tensor(out=AP(zc[zs], 2, [[1, 512]]), in0=PS(bc_), in1=AP(usb[zs], 0, [[1, 512]]), op=ALU.mult)),
                   r=[("ps", bc_), ("A", "usb", zs)], w=[zk])
            pr.add("act", (lambda e, j=j, zs=zs: e.copy(out=AP(zh, 2 * j, [[1, 2]]), in_=AP(zc[zs], 512, [[1, 2]]))), r=[zk], w=[("zh", j)])
            yk = ("A", "y", zs)
            pr.add("act", (lambda e, j=j, zs=zs: e.activation(out=AP(ybuf[zs], 0, [[1, 512]]), in_=AP(zc[zs], 2, [[1, 512]]), func=AF.Copy, scale=AP(convw, 3 * j + 2, [[1, 1]]))),
                   r=[zk, (CK, "convw")], w=[yk])
            pr.add("dve", (lambda e, j=j, zs=zs: e.scalar_tensor_tensor(out=AP(ybuf[zs], 0, [[1, 512]]), in0=AP(zc[zs], 1, [[1, 512]]), scalar=AP(convw, 3 * j + 1, [[1, 1]]),
                                                                      in1=AP(ybuf[zs], 0, [[1, 512]]), op0=ALU.mult, op1=ALU.add)),
                   r=[zk, yk, (CK, "convw")], w=[yk])
            pr.add("dve", (lambda e, j=j, zs=zs: e.scalar_tensor_tensor(out=AP(ybuf[zs], 0, [[1, 512]]), in0=AP(zc[zs], 0, [[1, 512]]), scalar=AP(convw, 3 * j, [[1, 1]]),
                                                                      in1=AP(ybuf[zs], 0, [[1, 512]]), op0=ALU.mult, op1=ALU.add)),
                   r=[zk, yk, (CK, "convw")], w=[yk])
            pr.add("dve", (lambda e, j=j, zs=zs, bb_=bb_: e.tensor_tensor(out=AP(obuf, j * 512, [[1, 512]]), in0=PS(bb_), in1=AP(ybuf[zs], 0, [[1, 512]]), op=ALU.mult)),
                   r=[("ps", bb_), yk], w=[("A", "o", j)])
            pr.add("act", (lambda e, j=j, zs=zs: e.activation(out=AP(sq[zs], 0, [[1, 512]]), in_=AP(obuf, j * 512, [[1, 512]]), func=AF.Square)),
                   r=[("A", "o", j)], w=[("A", "sq", zs)])

        def ssq_mm(j):
            zs = j % 2
            pr.add("pe", (lambda e, j=j, zs=zs: e.matmul(PS(BS), lhsT=AP(ones, 0, [[1, 128]]), rhs=AP(sq[zs], 0, [[1, 512]]), start=(j == 0), stop=(j == 7))),
                   r=[("A", "sq", zs), (CK, "ones")], w=[("ps", BS)])

        def conv_finish():
            pr.add("act", lambda e: e.activation(out=AP(rstd_c, 0, [[1, 512]]), in_=PS(BS), func=AF.Ln, scale=1.0 / 1024.0, bias=EPS_RMS),
                   r=[("ps", BS), (CK, "eps1")], w=[("A", "rstd_c")])
            pr.add("act", lambda e: e.activation(out=AP(rstd_c, 0, [[1, 512]]), in_=AP(rstd_c, 0, [[1, 512]]), func=AF.Exp, scale=-0.5),
                   r=[("A", "rstd_c")], w=[("A", "rstd_c")])
            for j in range(8):
                pr.add("dve", (lambda e, j=j: e.scalar_tensor_tensor(out=AP(mT, (8 + j) * 512, [[1, 512]]), in0=AP(obuf, j * 512, [[1, 512]]), scalar=AP(gconv, j, [[1, 1]]),
                                                                   in1=AP(rstd_c, 0, [[1, 512]]), op0=ALU.mult, op1=ALU.mult)),
                       r=[("A", "o", j), ("A", "rstd_c"), (CK, "gconv")], w=[("mT", 8 + j, tt) for tt in range(4)])

        att_state = {}

        def att_front(tt, half):
            g_prev = 4 * p + tt
            g_cur = g_prev + 1
            mprev = maskp0 if (p == 0 and tt == 0) else maskp
            info = []
            for kk in range(2):
                kv = 2 * half + kk
                p0 = 0 if kv < 2 else 64
                c = kv % 2
                ch0 = 4 * (kv % 2)
                pslot = (4 * tt + kv) % 3
                bs = [nb(rot), nb(rot)]
                for ci, tile in ((0, g_prev), (1, g_cur)):
                    pr.add("pe", (lambda e, b=bs[ci], c=c, tile=tile, p0=p0, ch0=ch0, tt=tt: e.matmul(
                        PS(b), lhsT=AP(KT, c * TOK + tile * 128, [[1, 128]], p0=p0, npart=64),
                        rhs=AP(qT, ch0 * 512 + tt * 128, [[512, 4], [1, 128]], p0=p0, npart=64), start=True, stop=True)),
                        r=[("KT", tile), ("A", "qT", tt)], w=[("ps", bs[ci])])
                for ci in range(2):
                    pr.add("act", (lambda e, b=bs[ci], ci=ci, pslot=pslot: e.activation(out=AP(pT[pslot], ci * 512, [[1, 512]]), in_=PS(b), func=AF.Exp, scale=ATTN_SCALE)),
                           r=[("ps", bs[ci])], w=[("A", "pT", pslot, ci)])
                    mk = mprev if ci == 0 else maskc
                    pr.add("dve", (lambda e, ci=ci, pslot=pslot, mk=mk: e.tensor_tensor(out=AP(pT[pslot], ci * 512, [[128, 4], [1, 128]]), in0=AP(pT[pslot], ci * 512, [[128, 4], [1, 128]]),
                                                                                      in1=AP(mk, 0, [[0, 4], [1, 128]]), op=ALU.mult)),
                           r=[("A", "pT", pslot, ci), (CK, cname[id(mk)])], w=[("A", "pT", pslot, ci)])
                info.append((kv, pslot))
            att_state[(tt, half)] = (info, g_prev, g_cur)

        def att_pv(tt, half):
            info, g_prev, g_cur = att_state[(tt, half)]
            for kk, (kv, pslot) in enumerate(info):
                for g4 in range(4):
                    for ci, tile in ((0, g_prev), (1, g_cur)):
                        pr.add("pe", (lambda e, g4=g4, ci=ci, tile=tile, kv=kv, pslot=pslot, b=OB[kk]: e.matmul(
                            PS(b, g4 * 65, 65), lhsT=AP(pT[pslot], ci * 512 + g4 * 128, [[1, 128]]), rhs=AP(V, tile * 260 + kv * 65, [[1, 65]]),
                            start=(ci == 0), stop=(ci == 1))),
                            r=[("A", "pT", pslot, ci), ("V", tile)], w=[("ps", OB[kk])])

        att_sm = {}

        def att_post(tt, half):
            asl = tt % 2
            if half == 0:
                att_sm[tt] = smalloc(2)
            o_ssq = att_sm[tt]
            o_den = smalloc(8)
            o_rden = smalloc(8)
            for kk in range(2):
                kv = 2 * half + kk
                pr.add("dve", (lambda e, b=OB[kk], kk=kk, kv=kv, o_den=o_den: e.tensor_tensor(out=AP(small, o_den + 4 * kk, [[1, 4]]), in0=AP(ps, b * 512 + 64, [[65, 4]]),
                                                                                           in1=AP(expsink, 4 * kv, [[1, 4]]), op=ALU.add)),
                       r=[("ps", OB[kk]), (CK, "sink")], w=[("A", "den", tt, half)])
            pr.add("dve", (lambda e, o_den=o_den, o_rden=o_rden: e.reciprocal(out=AP(small, o_rden, [[1, 8]]), in_=AP(small, o_den, [[1, 8]]))),
                   r=[("A", "den", tt, half)], w=[("A", "rden", tt, half)])
            for kk in range(2):
                kv = 2 * half + kk
                pr.add("dve", (lambda e, b=OB[kk], kk=kk, kv=kv, o_rden=o_rden, asl=asl: e.tensor_tensor(
                    out=AP(atok[asl], kv * 256, [[64, 4], [1, 64]]), in0=AP(ps, b * 512, [[65, 4], [1, 64]]),
                    in1=AP(small, o_rden + 4 * kk, [[1, 4], [0, 64]]), op=ALU.mult)),
                    r=[("ps", OB[kk]), ("A", "rden", tt, half)], w=[("A", "atok", asl, half)])
            pr.add("act", (lambda e, half=half, asl=asl, o_ssq=o_ssq: e.activation(out=AP(junk, 0, [[1, 512]]), in_=AP(atok[asl], half * 512, [[1, 512]]), func=AF.Square,
                                                                                 accum_out=AP(small, o_ssq + half, [[1, 1]]))),
                   r=[("A", "atok", asl, half)], w=[("A", "ssq", tt, half), ("A", "junk")])

        def att_final_a(tt):
            asl = tt % 2
            o_ssq = att_sm[tt]
            o_r = smalloc(2)
            pr.add("dve", (lambda e, o_ssq=o_ssq, o_r=o_r: e.tensor_tensor(out=AP(small, o_r, [[1, 1]]), in0=AP(small, o_ssq, [[1, 1]]), in1=AP(small, o_ssq + 1, [[1, 1]]), op=ALU.add)),
                   r=[("A", "ssq", tt, 0), ("A", "ssq", tt, 1)], w=[("A", "rs", tt)])
            pr.add("act", (lambda e, o_r=o_r: e.activation(out=AP(small, o_r, [[1, 1]]), in_=AP(small, o_r, [[1, 1]]), func=AF.Ln, scale=1.0 / 1024.0, bias=EPS_RMS)),
                   r=[("A", "rs", tt), (CK, "eps1")], w=[("A", "rs", tt)])
            pr.add("act", (lambda e, o_r=o_r: e.activation(out=AP(small, o_r + 1, [[1, 1]]), in_=AP(small, o_r, [[1, 1]]), func=AF.Exp, scale=-0.5)),
                   r=[("A", "rs", tt)], w=[("A", "rs2", tt)])
            pr.add("dve", (lambda e, o_r=o_r, asl=asl: e.scalar_tensor_tensor(out=AP(mtok[asl], 0, [[1, 1024]]), in0=AP(atok[asl], 0, [[1, 1024]]), scalar=AP(small, o_r + 1, [[1, 1]]),
                                                                            in1=AP(gattn, 0, [[1, 1024]]), op0=ALU.mult, op1=ALU.mult)),
                   r=[("A", "atok", asl, 0), ("A", "atok", asl, 1), ("A", "rs2", tt), (CK, "gattn")], w=[("A", "mtok", asl)])

        def att_final_b(tt):
            asl = tt % 2
            b = nb(rot)
            for j in range(8):
                pr.add("pe", (lambda e, j=j, b=b, asl=asl: e.transpose(out=AP(psb, b * 1024 + j * 128, [[1, 128]]), in_=AP(mtok[asl], j * 128, [[1, 128]]), identity=AP(ident, 0, [[1, 128]]))),
                       r=[("A", "mtok", asl), (CK, "ident")], w=[("ps", b)])
            pr.add("act", (lambda e, b=b, tt=tt: e.copy(out=AP(mT, tt * 128, [[512, 8], [1, 128]]), in_=AP(psb, b * 1024, [[128, 8], [1, 128]]))),
                   r=[("ps", b)], w=[("mT", j, tt) for j in range(8)])

        do_att = stop_after != "A4"
        for u in range(8):
            tt, half = u // 2, u % 2
            if do_att:
                att_front(tt, half)
            cbu_mm(u)
            if do_att:
                att_pv(tt, half)
            if u >= 1:
                ssq_mm(u - 1)
            if do_att and half == 0 and tt >= 1:
                att_final_b(tt - 1)
            conv_chain(u)
            if do_att:
                att_post(tt, half)
                if half == 1:
                    att_final_a(tt)
        ssq_mm(7)
        if do_att:
            att_final_b(3)
        conv_finish()

        if stop_after == "A4":
            break

        if stop_after == "B":
            break
        pr.barrier(exempt=("pe",))

        def ln_load(gd, bd):
            pr.add("sp", (lambda e, gd=gd: e.dma_start(out=AP(lngb, 0, [[1, 2048]]), in_=DAP(gd, 0, [[0, 128], [1, D]]))), w=[("lngb", 0)], dma=("lngb", 0))
            pr.add("sp", (lambda e, bd=bd: e.dma_start(out=AP(lngb, 2048, [[1, 2048]]), in_=DAP(bd, 0, [[0, 128], [1, D]]))), w=[("lngb", 1)], dma=("lngb", 1))

        def ln_stage_list(lsm, final):
            def stages(tt):
                hk = [("h1", tt, dt) for dt in range(4)]
                o = lsm[tt]
                S = lambda i: AP(small, o + i, [[1, 1]])
                k = lambda nm: ("lnk", nm, o)
                st = []
                st.append(lambda: pr.add("act", (lambda e: e.activation(out=AP(junk2, 0, [[1, 2048]]), in_=AP(h1, tt * 2048, [[1, 2048]]), func=AF.Square, accum_out=S(4))),
                                         r=hk, w=[k("sq"), "junk2"]))

                def tiny():
                    pr.add("dve", (lambda e: e.reduce_sum(out=S(5), in_=AP(small, o, [[1, 4]]), axis=mybir.AxisListType.X)), r=[("lns", tt, dt) for dt in range(4)], w=[k("tot")])
                    pr.add("dve", (lambda e: e.tensor_scalar(out=S(5), in0=S(5), scalar1=1.0 / D, scalar2=None, op0=ALU.mult)), r=[k("tot")], w=[k("mean")])
                    pr.add("dve", (lambda e: e.tensor_tensor(out=S(6), in0=S(5), in1=S(5), op=ALU.mult)), r=[k("mean")], w=[k("msq")])
                    pr.add("dve", (lambda e: e.scalar_tensor_tensor(out=S(6), in0=S(4), scalar=1.0 / D, in1=S(6), op0=ALU.mult, op1=ALU.subtract)), r=[k("sq"), k("msq")], w=[k("var")])
                    pr.add("act", (lambda e: e.activation(out=S(6), in_=S(6), func=AF.Ln, bias=EPS_LN)), r=[k("var"), (CK, "eps2")], w=[k("lnv")])
                    pr.add("act", (lambda e: e.activation(out=S(7), in_=S(6), func=AF.Exp, scale=-0.5)), r=[k("lnv")], w=[k("rstd")])
                    pr.add("dve", (lambda e: e.tensor_scalar(out=S(8), in0=S(5), scalar1=S(7), scalar2=-1.0, op0=ALU.mult, op1=ALU.mult)), r=[k("mean"), k("rstd")], w=[k("nmr")])
                st.append(tiny)
                def affine():
                    hb = lambda h: [("ps", 2 * h), ("ps", 2 * h + 1)]
                    hkh = lambda h: [("h1", tt, 2 * h), ("h1", tt, 2 * h + 1)]
                    for h in range(2):
                        pr.add("act", (lambda e, h=h: e.activation(out=AP(ps, h * 1024, [[1, 1024]]), in_=AP(h1, tt * 2048 + h * 1024, [[1, 1024]]), func=AF.Identity, scale=S(7), bias=S(8))),
                               r=hkh(h) + [k("rstd"), k("nmr")], w=hb(h))
                    for h in range(2):
                        pr.add("dve", (lambda e, h=h: e.tensor_tensor(out=AP(ps, h * 1024, [[1, 1024]]), in0=AP(ps, h * 1024, [[1, 1024]]), in1=AP(lngb, h * 1024, [[1, 1024]]), op=ALU.mult)),
                               r=[("lngb", 0)], w=hb(h))
                        pr.add("dve", (lambda e, h=h: e.tensor_tensor(out=AP(h1, tt * 2048 + h * 1024, [[1, 1024]]), in0=AP(ps, h * 1024, [[1, 1024]]), in1=AP(lngb, 2048 + h * 1024, [[1, 1024]]), op=ALU.add)),
                               r=[("lngb", 1)] + hb(h), w=hkh(h))
                st.append(affine)
                if final:
                    r0 = (4 * p + tt) * 128
                    st.append(lambda: pr.add("sp", (lambda e: e.dma_start(out=out_c[r0:r0 + 128, :], in_=AP(h1, tt * 2048, [[1, 2048]]))), r=hk, dma=("h1", tt)))
                else:
                    hs = tt % 2
                    b0, b1 = (4, 5) if tt % 2 == 0 else (6, 7)
                    st.append(lambda: pr.add("act", (lambda e: e.copy(out=AP(h1b[hs], 0, [[1, 2048]]), in_=AP(h1, tt * 2048, [[1, 2048]]))), r=hk, w=[("h1b", hs)]))

                    def tr():
                        for c in range(16):
                            b = b0 if c < 8 else b1
                            pr.add("pe", (lambda e, c=c, b=b: e.transpose(out=AP(psb, b * 1024 + (c % 8) * 128, [[1, 128]]), in_=AP(h1b[hs], c * 128, [[1, 128]]), identity=AP(ident, 0, [[1, 128]]))),
                                   r=[("h1b", hs), (CK, "ident")], w=[("ps", b)])
                    st.append(tr)

                    def ev():
                        pr.add("act", (lambda e: e.copy(out=AP(h1T, tt * 128, [[512, 8], [1, 128]]), in_=AP(psb, b0 * 1024, [[128, 8], [1, 128]]))), r=[("ps", b0)], w=[("h1T", tt, 0)])
                        pr.add("dve", (lambda e: e.tensor_copy(out=AP(h1T, 8 * 512 + tt * 128, [[512, 8], [1, 128]]), in_=AP(psb, b1 * 1024, [[128, 8], [1, 128]]))), r=[("ps", b1)], w=[("h1T", tt, 1)])
                    st.append(ev)
                return st

            return stages

        ln_load(ln1g_d, ln1b_d)
        ln_sm = [smalloc(12) for _ in range(4)]
        ln1_stages = ln_stage_list(ln_sm, final=False)
        ln1_st = [None] * 4
        for tt in range(4):
            r0 = (1 + 4 * p + tt) * 128
            pr.add("sp", (lambda e, tt=tt, r0=r0: e.dma_start(out=AP(h1, tt * 2048, [[1, 2048]]), in_=x_c[r0:r0 + 128, :])),
                   w=[("h1", tt, dt) for dt in range(4)], dma=("h1", tt))
        for dt in range(4):
            s = next_ws()
            pr.add("pool", (lambda e, dt=dt, s=s: e.dma_start(out=AP(wsl[s], 0, [[512, 16], [1, 512]]), in_=DAP(w_out, dt * 512, [[D, 128], [128 * D, 16], [1, 512]]))),
                   w=WK(s), dma=("w", s, 0))
            for tt in range(4):
                b = nb()
                for c in range(16):
                    pr.add("pe", (lambda e, c=c, b=b, s=s, tt=tt: e.matmul(PS(b), lhsT=AP(mT, c * 512 + tt * 128, [[1, 128]]), rhs=AP(wsl[s], c * 512, [[1, 512]]),
                                                                          start=(c == 0), stop=(c == 15))),
                           r=[("mT", c, tt)] + WK(s), w=[("ps", b)])
                pr.add("dve", (lambda e, b=b, tt=tt, dt=dt, oacc=ln_sm[tt] + dt: e.scalar_tensor_tensor(out=AP(h1, tt * 2048 + dt * 512, [[1, 512]]), in0=AP(h1, tt * 2048 + dt * 512, [[1, 512]]), scalar=ALPHA,
                                                                                  in1=PS(b), op0=ALU.mult, op1=ALU.add, accum_out=AP(small, oacc, [[1, 1]]))),
                       r=[("ps", b), ("h1", tt, dt)], w=[("h1", tt, dt), ("lns", tt, dt)])

        ln1_st = [ln1_stages(tt) for tt in range(4)]
        for wv in range(len(ln1_st[0]) + 3):
            for tt in range(4):
                si = wv - tt
                if 0 <= si < len(ln1_st[0]):
                    ln1_st[tt][si]()

        if stop_after == "C":
            break

        ln_load(ln2g_d, ln2b_d)
        ln_sm2 = [smalloc(12) for _ in range(4)]
        ln2_stages = ln_stage_list(ln_sm2, final=True)
        h1Tk = [("h1T", tt, hh) for tt in range(4) for hh in range(2)]
        for half in range(2):
            for fp in range(11):
                s = next_ws()
                col0 = (half * 22 + 2 * fp) * 128
                pr.add("pool", (lambda e, s=s, col0=col0: e.dma_start(out=AP(wsl[s], 0, [[512, 16], [1, 256]]), in_=DAP(w_gate, col0, [[DFF, 128], [128 * DFF, 16], [1, 256]]))),
                       w=[("w", s, 0)], dma=("w", s, 0))
                pr.add("pool", (lambda e, s=s, col0=col0: e.dma_start(out=AP(wsl[s], 256, [[512, 16], [1, 256]]), in_=DAP(w_up, col0, [[DFF, 128], [128 * DFF, 16], [1, 256]]))),
                       w=[("w", s, 1)], dma=("w", s, 1))
                for i in range(2):
                    fcl = 2 * fp + i
                    bg, bu = nb(), nb()
                    for gi, b in ((0, bg), (1, bu)):
                        for c in range(16):
                            pr.add("pe", (lambda e, c=c, b=b, s=s, gi=gi, i=i: e.matmul(PS(b), lhsT=AP(wsl[s], c * 512 + gi * 256 + i * 128, [[1, 128]]), rhs=AP(h1T, c * 512, [[1, 512]]),
                                                                                      start=(c == 0), stop=(c == 15))),
                                   r=h1Tk + [("w", s, gi)], w=[("ps", b)])
                    ss = fcl % 2
                    pr.add("act", (lambda e, bg=bg, ss=ss: e.activation(out=AP(sg[ss], 0, [[1, 512]]), in_=PS(bg), func=AF.Silu)), r=[("ps", bg)], w=[("sg", ss)])
                    pr.add("dve", (lambda e, bu=bu, ss=ss, fcl=fcl: e.tensor_tensor(out=AP(actT, fcl * 512, [[1, 512]]), in0=PS(bu), in1=AP(sg[ss], 0, [[1, 512]]), op=ALU.mult)),
                           r=[("ps", bu), ("sg", ss)], w=[("actT", fcl)])
            for dt in range(4):
                last = (half == 1 and dt == 3)
                bk = [nb(range(4, 8)) for _ in range(4)] if last else [nb() for _ in range(4)]
                for piece in range(2):
                    s = next_ws()
                    f0 = (half * 22 + piece * 11) * 128
                    pr.add("pool", (lambda e, s=s, f0=f0, dt=dt: e.dma_start(out=AP(wsl[s], 0, [[512, 11], [1, 512]]), in_=DAP(w_down, f0 * D + dt * 512, [[D, 128], [128 * D, 11], [1, 512]]))),
                           w=WK(s), dma=("w", s, 0))
                    for tt in range(4):
                        for k in range(11):
                            fcl = piece * 11 + k
                            pr.add("pe", (lambda e, k=k, fcl=fcl, tt=tt, s=s, b=bk[tt], piece=piece: e.matmul(PS(b), lhsT=AP(actT, fcl * 512 + tt * 128, [[1, 128]]), rhs=AP(wsl[s], k * 512, [[1, 512]]),
                                                                                                          start=(piece == 0 and k == 0), stop=(piece == 1 and k == 10))),
                                   r=[("actT", fcl)] + WK(s), w=[("ps", bk[tt])])
                for tt in range(4):
                    if half == 0:
                        pr.add("dve", (lambda e, b=bk[tt], tt=tt, dt=dt: e.scalar_tensor_tensor(out=AP(h1, tt * 2048 + dt * 512, [[1, 512]]), in0=AP(h1, tt * 2048 + dt * 512, [[1, 512]]), scalar=ALPHA,
                                                                                              in1=PS(b), op0=ALU.mult, op1=ALU.add)),
                               r=[("ps", bk[tt]), ("h1", tt, dt)], w=[("h1", tt, dt)])
                    else:
                        pr.add("dve", (lambda e, b=bk[tt], tt=tt, dt=dt, oacc=ln_sm2[tt] + dt: e.scalar_tensor_tensor(out=AP(h1, tt * 2048 + dt * 512, [[1, 512]]), in0=AP(h1, tt * 2048 + dt * 512, [[1, 512]]), scalar=1.0,
                                                                                              in1=PS(b), op0=ALU.mult, op1=ALU.add, accum_out=AP(small, oacc, [[1, 1]]))),
                               r=[("ps", bk[tt]), ("h1", tt, dt)], w=[("h1", tt, dt), ("lns", tt, dt)])
        tail_start = len(pr.ops)
        ln2_st = [ln2_stages(tt) for tt in range(4)]
        for wv in range(len(ln2_st[0]) + 3):
            for tt in range(4):
                si = wv - tt
                if 0 <= si < len(ln2_st[0]):
                    ln2_st[tt][si]()
        pending_tail = pr.ops[tail_start:]
        del pr.ops[tail_start:]

    if pending_tail is not None:
        pr.ops.extend(pending_tail)
        pending_tail = None
    if debug:
        pr.barrier()
        for name, (t, dt) in dumps.items():
            dd = nc.dram_tensor("dbg_" + name, [128, t.shape[1]], dt, kind="ExternalOutput")
            pr.add("sp", (lambda e, t=t, dd=dd: e.dma_start(out=dd.ap(), in_=AP(t, 0, [[1, t.shape[1]]]))), dma=("dbg", name))
    pr.emit()
    return nc, pr


def _host_consts():
    bf = ml_dtypes.bfloat16
    ident = np.eye(128, dtype=np.float32).astype(bf)
    k = np.arange(128)[:, None]
    q = np.arange(128)[None, :]
    maskp = (k > q).astype(np.float32).astype(bf)
    maskc = (k <= q).astype(np.float32).astype(bf)
    inv = (np.float32(500000.0) ** (-np.arange(0, 16, 2, dtype=np.float32) / np.float32(16))).astype(np.float32)
    invf = np.broadcast_to(inv[None, :], (128, 8)).copy()
    return ident, maskp, maskc, invf


_CACHE = {}


def make_in_maps(x, positions, w_in, conv_w, sinks, g_attn, g_conv, w_out, ln1_g, ln1_b, w_gate, w_up, w_down, ln2_g, ln2_b, cores=range(8)):
    ident, maskp, maskc, invf = _host_consts()
    f = lambda a: np.ascontiguousarray(np.asarray(a, dtype=np.float32))
    x = np.asarray(x)
    positions = np.asarray(positions)
    shared = {
        "w_in": f(w_in[0]), "w_out": f(w_out[0]), "w_gate": f(w_gate[0]), "w_up": f(w_up[0]), "w_down": f(w_down[0]),
        "convw": np.ascontiguousarray(np.asarray(conv_w[0], np.float32).reshape(3, 8, 128).transpose(2, 1, 0).reshape(128, 24)),
        "gconv": np.ascontiguousarray(np.asarray(g_conv[0], np.float32).reshape(8, 128).T),
        "gattn": f(g_attn[0]).reshape(1, 1024), "sinks": f(sinks[0]).reshape(1, 16),
        "ln1g": f(ln1_g[0]).reshape(1, D), "ln1b": f(ln1_b[0]).reshape(1, D), "ln2g": f(ln2_g[0]).reshape(1, D), "ln2b": f(ln2_b[0]).reshape(1, D),
        "ident": ident, "maskp": maskp, "maskc": maskc, "invf": invf,
    }
    maps = []
    for core in cores:
        b, half = core // 2, core % 2
        s0 = half * 2048
        xc = np.zeros((TOK, D), np.float32)
        pc = np.zeros((TOK,), np.int32)
        xc[128:] = x[b, s0:s0 + 2048]
        pc[128:] = positions[b, s0:s0 + 2048]
        if half == 1:
            xc[:128] = x[b, s0 - 128:s0]
            pc[:128] = positions[b, s0 - 128:s0]
        m = dict(shared)
        m["x_c"] = xc
        m["pos_c"] = np.ascontiguousarray(pc.reshape(NTILE, 128).T)
        m["maskp0"] = maskp if half == 1 else np.zeros_like(maskp)
        maps.append(m)
    return maps


def kernel(**inputs):
    if "nc" not in _CACHE:
        _CACHE["nc"] = build()[0]
    nc = _CACHE["nc"]
    maps = make_in_maps(**inputs)
    res = run_bass_kernel_spmd(nc, maps, core_ids=list(range(8)))
    out = np.empty((4, 4096, D), np.float32)
    for core in range(8):
        b, half = core // 2, core % 2
        out[b, half * 2048:(half + 1) * 2048] = np.asarray(res.results[core]["out_c"])
    return out
```

```python
import numpy as np
import ml_dtypes
import concourse.bass as bass
import concourse.mybir as mybir
from concourse.bass_utils import run_bass_kernel_spmd

F32 = mybir.dt.float32
BF16 = mybir.dt.bfloat16
I32 = mybir.dt.int32
U8 = mybir.dt.uint8
ALU = mybir.AluOpType
AF = mybir.ActivationFunctionType

D = 2048
DFF = 5632
INW = 4608
NCH = 16
T = 512
NPASS = 4
NTT = 4
NTILE = 17
TOK = NTILE * 128
ALPHA = 2.0 ** 0.25
ATTN_SCALE = 0.125
LN_EPS = 1e-5
RMS_EPS = 1e-6
TWO_PI = 6.283185307179586
C1 = 6.28125
C2 = float(np.float32(TWO_PI - C1))
C3 = float(np.float32(TWO_PI - C1 - C2))
NWS = 3


def _dsize(dt):
    return {F32: 4, BF16: 2, I32: 4, U8: 1}[dt]


class Op:
    __slots__ = ("eng", "fn", "r", "w", "dma", "inc", "seq", "barrier")

    def __init__(self, eng, fn, r, w, dma):
        self.eng, self.fn, self.r, self.w, self.dma = eng, fn, r, w, dma
        self.inc = False
        self.seq = 0
        self.barrier = False


class Prog:
    PART = ("pe", "act", "dve", "sp")

    def __init__(self, nc):
        self.nc = nc
        self.ops = []
        self.E = {"pe": nc.tensor, "act": nc.scalar, "dve": nc.vector, "pool": nc.gpsimd, "sp": nc.sync}

    def add(self, eng, fn, r=(), w=(), dma=None):
        r, w = tuple(r), tuple(w)
        w = w + tuple(k for k in r if isinstance(k, tuple) and k[0] == "ps" and k not in w)
        self.ops.append(Op(eng, fn, r, w, dma))

    def barrier(self, exempt=()):
        o = Op("barrier", None, (), (), None)
        o.barrier = True
        o.r = tuple(exempt)
        self.ops.append(o)

    def emit(self, final_wait_eng="sp"):
        nc, ops = self.nc, self.ops
        n = len(ops)
        last_w = {}
        rd_eng = {}
        rd_dma = {}
        deps = [None] * n
        last_on_eng = {}
        last_dma_key = {}
        pending = {}
        for i, op in enumerate(ops):
            if op.barrier:
                fr = set(last_on_eng.values()) | set(last_dma_key.values())
                for e in self.PART:
                    if e not in op.r:
                        pending[e] = pending.get(e, set()) | fr
                for dct in (last_w, rd_eng, rd_dma):
                    for k in [k for k in dct if isinstance(k, tuple) and k and k[0] == "A"]:
                        del dct[k]
                continue
            d = {}
            for k in op.r:
                j = last_w.get(k)
                if j is not None:
                    d[j] = True
            for k in op.w:
                j = last_w.get(k)
                if j is not None:
                    d.setdefault(j, False)
                for j in rd_eng.get(k, {}).values():
                    d.setdefault(j, False)
                for j in rd_dma.get(k, ()):
                    d.setdefault(j, False)
            if op.eng in pending:
                for j in pending.pop(op.eng):
                    d[j] = True
            d.pop(i, None)
            deps[i] = d
            for k in op.r:
                if op.dma is not None:
                    rd_dma.setdefault(k, []).append(i)
                else:
                    rd_eng.setdefault(k, {})[op.eng] = i
            for k in op.w:
                last_w[k] = i
                rd_eng[k] = {}
                rd_dma[k] = []
            if op.dma is not None:
                if op.eng != "pool":
                    last_dma_key[op.dma] = i
            elif op.eng in self.PART:
                last_on_eng[op.eng] = i
        need = [None] * n
        for i, op in enumerate(ops):
            if op.barrier:
                continue
            lst = []
            for j, raw in deps[i].items():
                pj = ops[j]
                if pj.dma is not None:
                    lst.append(j)
                elif pj.eng == op.eng:
                    if op.dma is not None:
                        lst.append(j)
                        pj.inc = True
                    elif op.eng == "pe":
                        continue
                    else:
                        lst.append(j)
                        pj.inc = True
                else:
                    lst.append(j)
                    pj.inc = True
            need[i] = lst
        self.need_dbg = need
        cnt = {}
        for op in ops:
            if op.barrier:
                continue
            if op.dma is not None:
                key = ("dma", op.dma)
                cnt[key] = cnt.get(key, 0) + 16
                op.seq = cnt[key]
            elif op.inc:
                cnt[op.eng] = cnt.get(op.eng, 0) + 1
                op.seq = cnt[op.eng]
        for k, v in cnt.items():
            assert v < 32000, (k, v)
        sems = {}

        def sem_of(op):
            key = ("dma", op.dma) if op.dma is not None else op.eng
            if key not in sems:
                sems[key] = nc.alloc_semaphore(f"s{len(sems)}")
            return sems[key]

        waited = {e: {} for e in self.E}
        nwait = 0
        for i, op in enumerate(ops):
            if op.barrier:
                continue
            e = self.E[op.eng]
            wl = {}
            for j in need[i]:
                pj = ops[j]
                s = sem_of(pj)
                if pj.seq > wl.get(s, (0, None))[0]:
                    wl[s] = (pj.seq, s)
            for s, (v, _) in wl.items():
                if waited[op.eng].get(s, 0) < v:
                    e.wait_ge(s, v)
                    waited[op.eng][s] = v
                    nwait += 1
            ins = op.fn(e)
            if op.dma is not None:
                ins.then_inc(sem_of(op), 16)
            elif op.inc:
                ins.then_inc(sem_of(op), 1)
        fe = self.E[final_wait_eng]
        for key, v in cnt.items():
            if isinstance(key, tuple) and key[0] == "dma":
                fe.wait_ge(sems[key], v)
        self.stats = dict(n_ops=n, n_wait=nwait, n_sems=len(sems), cnt={str(k): v for k, v in cnt.items()})


def build(debug=False, stop_after=None, npass=NPASS):
    nc = bass.Bass("TRN2", target_bir_lowering=False)
    pr = Prog(nc)

    def din(name, shape, dt):
        return nc.dram_tensor(name, list(shape), dt, kind="ExternalInput")

    x_c = din("x_c", [TOK, D], F32)
    pos_c = din("pos_c", [128, NTILE], I32)
    w_in = din("w_in", [D, INW], F32)
    w_out = din("w_out", [D, D], F32)
    w_gate = din("w_gate", [D, DFF], F32)
    w_up = din("w_up", [D, DFF], F32)
    w_down = din("w_down", [DFF, D], F32)
    convw_d = din("convw", [128, 24], F32)
    gconv_d = din("gconv", [128, 8], F32)
    gattn_d = din("gattn", [1, 1024], F32)
    sinks_d = din("sinks", [1, 16], F32)
    ln1g_d = din("ln1g", [1, D], F32)
    ln1b_d = din("ln1b", [1, D], F32)
    ln2g_d = din("ln2g", [1, D], F32)
    ln2b_d = din("ln2b", [1, D], F32)
    ident_d = din("ident", [128, 128], BF16)
    maskp_d = din("maskp", [128, 128], BF16)
    maskc_d = din("maskc", [128, 128], BF16)
    maskp0_d = din("maskp0", [128, 128], BF16)
    invf_d = din("invf", [128, 16], F32)
    out_c = nc.dram_tensor("out_c", [2048, D], F32, kind="ExternalOutput")

    total = int(nc.sbuf_bytes_remaining) - 64
    slab = nc.alloc_sbuf_tensor("slab", [128, total], U8)
    base = int(nc.lookup_mloc(slab).addr)
    state = {"off": base, "limit": base + total}

    def sb(name, free, dt):
        nb_ = (free * _dsize(dt) + 31) // 32 * 32
        h = nc.alloc_sbuf_tensor_at(name, [128, free], dt, offset=state["off"])
        state["off"] += nb_
        assert state["off"] <= state["limit"], (name, state["off"] - base, total)
        return h

    ident = sb("ident", 128, BF16)
    ones = sb("ones", 128, BF16)
    maskp = sb("maskp", 128, BF16)
    maskc = sb("maskc", 128, BF16)
    maskp0 = sb("maskp0", 128, BF16)
    convw = sb("convw", 24, F32)
    gconv = sb("gconv", 8, F32)
    gattn = sb("gattn", 1024, F32)
    expsink = sb("expsink", 16, F32)
    cs_cos = sb("cs_cos", NTILE * 8, F32)
    cs_sin = sb("cs_sin", NTILE * 8, F32)
    cs_nsin = sb("cs_nsin", NTILE * 8, F32)
    KT = sb("KT", 2 * TOK, BF16)
    V = sb("V", NTILE * 260, BF16)
    zh = sb("zh", 16, F32)
    wsl = [sb(f"w{i}", 8192, BF16) for i in range(NWS)]
    xb = [sb(f"xb{i}", 2048, BF16) for i in range(4)]
    mT = sb("mT", 16 * 512, BF16)
    small = sb("small", 384, F32)
    fence = sb("fence", 8, F32)
    arena = state["off"]
    G0 = 51200
    xT = sb("xT", 16 * 640, BF16)
    qtok = sb("qtok", 5 * 1024, BF16)
    ktok = sb("ktok", 5 * 256, BF16)
    qT = sb("qT", 8 * 512, BF16)
    ropet = [sb(f"ropet{i}", 256, F32) for i in range(2)]
    ab0_end = state["off"]
    state["off"] = arena + G0
    usb = [sb(f"usb{i}", 512, F32) for i in range(2)]
    zc = [sb(f"zc{i}", 514, F32) for i in range(2)]
    ybuf = [sb(f"y{i}", 512, F32) for i in range(2)]
    obuf = sb("obuf", 8 * 512, F32)
    sq = [sb(f"sq{i}", 512, BF16) for i in range(2)]
    rstd_c = sb("rstd_c", 512, F32)
    pT = [sb(f"pT{i}", 1024, BF16) for i in range(3)]
    atok = [sb(f"atok{i}", 1024, F32) for i in range(2)]
    mtok = [sb(f"mtok{i}", 1024, BF16) for i in range(2)]
    junk = sb("junk", 512, BF16)
    ab_end = state["off"]
    state["off"] = arena
    h1b = [sb(f"h1b{i}", 2048, BF16) for i in range(2)]
    h1T = sb("h1T", 16 * 512, BF16)
    actT = sb("actT", 22 * 512, BF16)
    sg = [sb(f"sg{i}", 512, F32) for i in range(2)]
    cd0_end = state["off"]
    assert ab0_end <= arena + G0 and cd0_end <= arena + G0, (ab0_end - arena, cd0_end - arena, G0)
    state["off"] = arena + G0
    lngb = sb("lngb", 2 * 2048, F32)
    h1 = sb("h1", 4 * 2048, F32)
    junk2 = sb("junk2", 2048, BF16)
    cd_end = state["off"]
    state["off"] = max(ab_end, cd_end)

    ps = nc.alloc_psum_tensor("ps", [128, 8 * 512], F32)
    psb = ps.bitcast(BF16)

    def AP(t, off, dims, p0=0, npart=128):
        Fr = t.shape[1]
        return bass.AP(t, p0 * Fr + off, [[Fr, npart]] + [list(x) for x in dims])

    def DAP(t, off, dims):
        return bass.AP(t, off, [list(x) for x in dims])

    def PS(b, off=0, n=512):
        return AP(ps, b * 512 + off, [[1, n]])

    bank_state = {"i": 0}

    def nb(allowed=range(8)):
        allowed = list(allowed)
        while True:
            b = bank_state["i"] % 8
            bank_state["i"] += 1
            if b in allowed:
                return b

    sm_next = {"i": 0}

    def smalloc(n=1):
        o = sm_next["i"]
        sm_next["i"] += n
        assert sm_next["i"] <= 384
        return o

    CK = "const"
    dumps = {}
    if debug:
        dumps = {"KT": (KT, BF16), "V": (V, BF16), "mT": (mT, BF16), "cs_cos": (cs_cos, F32), "cs_sin": (cs_sin, F32)}
        if stop_after in ("A1", "A2", "A3", "A4", "B"):
            dumps.update({"xT": (xT, BF16), "qT": (qT, BF16), "qtok": (qtok, BF16), "ktok": (ktok, BF16), "obuf": (obuf, F32)})
        else:
            dumps.update({"h1": (h1, F32), "h1T": (h1T, BF16), "small": (small, F32)})
        for name, (t, dt) in dumps.items():
            if name.startswith("cs_"):
                continue
            pr.add("dve", (lambda e, t=t: e.memset(AP(t, 0, [[1, t.shape[1]]]), 0.0)))
        pr.barrier()
    cname = {}
    for (nm, dst, src, nfree) in (("ident", ident, ident_d, 128), ("maskp", maskp, maskp_d, 128), ("maskc", maskc, maskc_d, 128),
                                  ("maskp0", maskp0, maskp0_d, 128), ("convw", convw, convw_d, 24), ("gconv", gconv, gconv_d, 8)):
        cname[id(dst)] = nm
        pr.add("sp", (lambda e, dst=dst, src=src, nfree=nfree: e.dma_start(out=AP(dst, 0, [[1, nfree]]), in_=src.ap())),
               w=[(CK, nm)], dma=("c", nm))
    pr.add("sp", lambda e: e.dma_start(out=AP(gattn, 0, [[1, 1024]]), in_=DAP(gattn_d, 0, [[0, 128], [1, 1024]])),
           w=[(CK, "gattn")], dma=("c", "gattn"))
    pr.add("sp", lambda e: e.dma_start(out=AP(expsink, 0, [[1, 16]]), in_=DAP(sinks_d, 0, [[0, 128], [1, 16]])),
           w=[(CK, "sink")], dma=("c", "sink"))
    pr.add("act", lambda e: e.activation(out=AP(expsink, 0, [[1, 16]]), in_=AP(expsink, 0, [[1, 16]]), func=AF.Exp),
           r=[(CK, "sink")], w=[(CK, "sink")])
    pr.add("dve", lambda e: e.memset(AP(ones, 0, [[1, 128]]), 1.0), w=[(CK, "ones")])
    pr.add("dve", lambda e: e.memset(AP(V, 64, [[65, NTILE * 4]]), 1.0), w=[("V", t) for t in range(NTILE)])
    o_eps_rms = smalloc()
    o_eps_ln = smalloc()
    pr.add("dve", lambda e: e.memset(AP(small, o_eps_rms, [[1, 1]]), RMS_EPS), w=[(CK, "eps1")])
    pr.add("dve", lambda e: e.memset(AP(small, o_eps_ln, [[1, 1]]), LN_EPS), w=[(CK, "eps2")])
    EPS_RMS = AP(small, o_eps_rms, [[1, 1]])
    EPS_LN = AP(small, o_eps_ln, [[1, 1]])

    NA = NTILE * 8
    _save_off = state["off"]
    state["off"] = arena + G0
    posi = sb("posi", NTILE, I32)
    posf = sb("posf", NTILE, F32)
    invf = sb("invf", 16, F32)
    alo = sb("alo", NTILE * 8, F32)
    ang = sb("ang", NA, F32)
    kf = sb("kf", NA, F32)
    ki = sb("ki", NA, I32)
    rr = sb("rr", NA, F32)
    rc = sb("rc", NA, F32)
    mm_ = sb("mm_", NA, F32)
    state["off"] = _save_off
    pr.add("sp", lambda e: e.dma_start(out=AP(posi, 0, [[1, NTILE]]), in_=pos_c.ap()), w=["posi"], dma=("c", "posi"))
    pr.add("sp", lambda e: e.dma_start(out=AP(invf, 0, [[1, 16]]), in_=invf_d.ap()), w=["invf"], dma=("c", "invf"))
    pr.add("dve", lambda e: e.tensor_copy(out=AP(posf, 0, [[1, NTILE]]), in_=AP(posi, 0, [[1, NTILE]])), r=["posi"], w=["posf"])
    pr.add("dve", lambda e: e.tensor_tensor(out=AP(ang, 0, [[8, NTILE], [1, 8]]), in0=AP(posf, 0, [[1, NTILE], [0, 8]]),
                                            in1=AP(invf, 0, [[0, NTILE], [1, 8]]), op=ALU.mult), r=["posf", "invf"], w=["ang"])
    pr.add("dve", lambda e: e.tensor_tensor(out=AP(alo, 0, [[8, NTILE], [1, 8]]), in0=AP(posf, 0, [[1, NTILE], [0, 8]]),
                                            in1=AP(invf, 8, [[0, NTILE], [1, 8]]), op=ALU.mult), r=["posf", "invf"], w=["alo"])
    A1 = lambda t: AP(t, 0, [[1, NA]])
    pr.add("dve", lambda e: e.tensor_scalar(out=A1(kf), in0=A1(ang), scalar1=1.0 / TWO_PI, scalar2=None, op0=ALU.mult), r=["ang"], w=["kf"])
    pr.add("dve", lambda e: e.tensor_copy(out=A1(ki), in_=A1(kf)), r=["kf"], w=["ki"])
    pr.add("dve", lambda e: e.tensor_copy(out=A1(kf), in_=A1(ki)), r=["ki"], w=["kf"])
    pr.add("dve", lambda e: e.scalar_tensor_tensor(out=A1(rr), in0=A1(kf), scalar=-C1, in1=A1(ang), op0=ALU.mult, op1=ALU.add), r=["kf", "ang"], w=["rr"])
    pr.add("dve", lambda e: e.scalar_tensor_tensor(out=A1(rr), in0=A1(kf), scalar=-C2, in1=A1(rr), op0=ALU.mult, op1=ALU.add), r=["kf", "rr"], w=["rr"])
    pr.add("dve", lambda e: e.scalar_tensor_tensor(out=A1(rr), in0=A1(kf), scalar=-C3, in1=A1(rr), op0=ALU.mult, op1=ALU.add), r=["kf", "rr"], w=["rr"])
    pr.add("dve", lambda e: e.tensor_tensor(out=A1(rr), in0=A1(rr), in1=A1(alo), op=ALU.add), r=["rr", "alo"], w=["rr"])

    def wrap(t, key):
        pr.add("dve", lambda e: e.tensor_scalar(out=A1(mm_), in0=A1(t), scalar1=np.pi, scalar2=-TWO_PI, op0=ALU.is_gt, op1=ALU.mult), r=[key], w=["mm_"])
        pr.add("dve", lambda e: e.tensor_tensor(out=A1(t), in0=A1(t), in1=A1(mm_), op=ALU.add), r=[key, "mm_"], w=[key])
        pr.add("dve", lambda e: e.tensor_scalar(out=A1(mm_), in0=A1(t), scalar1=-np.pi, scalar2=TWO_PI, op0=ALU.is_lt, op1=ALU.mult), r=[key], w=["mm_"])
        pr.add("dve", lambda e: e.tensor_tensor(out=A1(t), in0=A1(t), in1=A1(mm_), op=ALU.add), r=[key, "mm_"], w=[key])
        pr.add("dve", lambda e: e.tensor_scalar(out=A1(t), in0=A1(t), scalar1=3.1415925, scalar2=-3.1415925, op0=ALU.min, op1=ALU.max), r=[key], w=[key])

    wrap(rr, "rr")
    pr.add("dve", lambda e: e.tensor_scalar(out=A1(rc), in0=A1(rr), scalar1=float(np.pi / 2), scalar2=None, op0=ALU.add), r=["rr"], w=["rc"])
    wrap(rc, "rc")
    pr.add("act", lambda e: e.activation(out=A1(cs_sin), in_=A1(rr), func=AF.Sin), r=["rr"], w=["cs"])
    pr.add("act", lambda e: e.activation(out=A1(cs_cos), in_=A1(rc), func=AF.Sin), r=["rc"], w=["cs"])
    pr.add("act", lambda e: e.mul(out=A1(cs_nsin), in_=A1(cs_sin), mul=-1.0), r=["cs"], w=["cs"])

    ws_i = {"i": 0}
    WK = lambda s_: [("w", s_, 0), ("w", s_, 1), ("w", s_, 2)]

    def next_ws():
        s_ = ws_i["i"] % NWS
        ws_i["i"] += 1
        return s_

    sm_base = sm_next["i"]

    import os as _os
    pending_tail = None
    for p in range(npass if stop_after != "K" else 0):
        tiles = ([0] if p == 0 else []) + [1 + 4 * p + i for i in range(4)]
        own = tiles[-4:]
        sm_next["i"] = sm_base

        def xcol(tile):
            return 0 if tile == 0 else 128 + 128 * (tile - (1 + 4 * p))

        def xTkeys(tile):
            return [("A", "xT", tile, 0), ("A", "xT", tile, 1)]

        def capture(fn):
            old = pr.ops
            pr.ops = []
            fn()
            lst = pr.ops
            pr.ops = old
            return lst

        abank_state = {"r": range(8) if p == 0 else range(4, 8)}
        a_mark = {"i": None}

        def block_a():
            wq_slots = []

            def wq_loads(cgs):
                for cg in cgs:
                    s = next_ws()
                    wq_slots.append(s)
                    pr.add("pool", (lambda e, cg=cg, s=s: e.dma_start(out=AP(wsl[s], 0, [[512, 16], [1, 512]]),
                                                                      in_=DAP(w_in, cg * 512, [[INW, 128], [128 * INW, 16], [1, 512]]))),
                           w=WK(s), dma=("w", s, 0))
            for ti, tile in enumerate(tiles):
                s = (tile) % 4
                pr.add("pool", (lambda e, tile=tile, s=s: e.dma_start(out=AP(xb[s], 0, [[1, 2048]]), in_=x_c[tile * 128:(tile + 1) * 128, :])),
                       w=[("xb", s)], dma=("xb", s))
                if ti == 1:
                    wq_loads([0])
                if ti == len(tiles) - 1:
                    wq_loads([1, 2])
                b0, b1 = nb(abank_state["r"]), nb(abank_state["r"])
                for c in range(16):
                    b = b0 if c < 8 else b1
                    pr.add("pe", (lambda e, c=c, b=b, s=s: e.transpose(out=AP(psb, b * 1024 + (c % 8) * 128, [[1, 128]]),
                                                                      in_=AP(xb[s], c * 128, [[1, 128]]), identity=AP(ident, 0, [[1, 128]]))),
                           r=[("xb", s), (CK, "ident")], w=[("ps", b)])
                col = xcol(tile)
                pr.add("act", (lambda e, b0=b0, col=col: e.copy(out=AP(xT, col, [[640, 8], [1, 128]]), in_=AP(psb, b0 * 1024, [[128, 8], [1, 128]]))),
                       r=[("ps", b0)], w=[("A", "xT", tile, 0)])
                pr.add("dve", (lambda e, b1=b1, col=col: e.tensor_copy(out=AP(xT, 8 * 640 + col, [[640, 8], [1, 128]]), in_=AP(psb, b1 * 1024, [[128, 8], [1, 128]]))),
                       r=[("ps", b1)], w=[("A", "xT", tile, 1)])

            for cg in range(3):
                if cg == 2:
                    abank_state["r"] = range(8)
                    a_mark["i"] = len(pr.ops)
                s = wq_slots[cg]
                for tile in (tiles if cg == 2 else own):
                    b = nb(abank_state["r"])
                    col = xcol(tile)
                    for c in range(16):
                        pr.add("pe", (lambda e, c=c, b=b, s=s, col=col: e.matmul(PS(b), lhsT=AP(xT, c * 640 + col, [[1, 128]]), rhs=AP(wsl[s], c * 512, [[1, 512]]),
                                                                                start=(c == 0), stop=(c == 15))),
                               r=xTkeys(tile) + WK(s), w=[("ps", b)])
                    tl = tile - (1 + 4 * p) + 1
                    if cg < 2:
                        qo = tl * 1024 + cg * 64
                        pr.add("act", (lambda e, b=b, qo=qo: e.copy(out=AP(qtok, qo + 16, [[128, 8], [1, 48]]), in_=AP(ps, b * 512 + 16, [[64, 8], [1, 48]]))),
                               r=[("ps", b)], w=[("A", "qtok", tl, cg, "p")])
                        rope_src = lambda off, b=b: AP(ps, b * 512 + off, [[64, 8], [1, 8]])
                        rope_dst = lambda off, qo=qo: AP(qtok, qo + off, [[128, 8], [1, 8]])
                        nh = 8
                        rkey = ("A", "qtok", tl, cg, "r")
                    else:
                        ko = tl * 256
                        pr.add("act", (lambda e, b=b, ko=ko: e.copy(out=AP(ktok, ko + 16, [[64, 2], [128, 2], [1, 48]]), in_=AP(ps, b * 512 + 16, [[128, 2], [64, 2], [1, 48]]))),
                               r=[("ps", b)], w=[("A", "ktok", tl, "p")])
                        pr.add("act", (lambda e, b=b, tile=tile: e.copy(out=AP(V, tile * 260, [[65, 4], [1, 64]]), in_=AP(ps, b * 512 + 256, [[64, 4], [1, 64]]))),
                               r=[("ps", b)], w=[("V", tile)])
                        rope_src = lambda off, b=b: AP(ps, b * 512 + off, [[128, 2], [64, 2], [1, 8]])
                        rope_dst = lambda off, ko=ko: AP(ktok, ko + off, [[64, 2], [128, 2], [1, 8]])
                        nh = 4
                        rkey = ("A", "ktok", tl, "r")
                    rs = (tile + cg) % 2
                    if nh == 8:
                        bc = lambda t_, tile=tile: AP(t_, tile * 8, [[0, 8], [1, 8]])
                        tmp = lambda off, rs=rs: AP(ropet[rs], off, [[16, 8], [1, 8]])
                    else:
                        bc = lambda t_, tile=tile: AP(t_, tile * 8, [[0, 2], [0, 2], [1, 8]])
                        tmp = lambda off, rs=rs: AP(ropet[rs], off, [[32, 2], [16, 2], [1, 8]])
                    tk = ("A", "ropet", rs)
                    pr.add("dve", (lambda e, rope_src=rope_src, bc=bc, tmp=tmp: e.tensor_tensor(out=tmp(0), in0=rope_src(8), in1=bc(cs_nsin), op=ALU.mult)),
                           r=[("ps", b), "cs"], w=[tk])
                    pr.add("dve", (lambda e, rope_src=rope_src, bc=bc, tmp=tmp: e.tensor_tensor(out=tmp(8), in0=rope_src(0), in1=bc(cs_sin), op=ALU.mult)),
                           r=[("ps", b), "cs"], w=[tk])
                    pr.add("dve", (lambda e, rope_src=rope_src, bc=bc, tmp=tmp: e.tensor_tensor(out=tmp(128), in0=rope_src(0), in1=bc(cs_cos), op=ALU.mult)),
                           r=[("ps", b), "cs"], w=[tk])
                    pr.add("dve", (lambda e, rope_src=rope_src, bc=bc, tmp=tmp: e.tensor_tensor(out=tmp(136), in0=rope_src(8), in1=bc(cs_cos), op=ALU.mult)),
                           r=[("ps", b), "cs"], w=[tk])
                    pr.add("dve", (lambda e, rope_dst=rope_dst, tmp=tmp: e.tensor_tensor(out=rope_dst(0), in0=tmp(0), in1=tmp(128), op=ALU.add)),
                           r=[tk], w=[rkey])
                    pr.add("dve", (lambda e, rope_dst=rope_dst, tmp=tmp: e.tensor_tensor(out=rope_dst(8), in0=tmp(8), in1=tmp(136), op=ALU.add)),
                           r=[tk], w=[rkey])

            for tile in tiles:
                tl = tile - (1 + 4 * p) + 1
                if tile != 0:
                    b = nb(abank_state["r"])
                    for j in range(8):
                        pr.add("pe", (lambda e, j=j, b=b, tl=tl: e.transpose(out=AP(psb, b * 1024 + j * 128, [[1, 128]]), in_=AP(qtok, tl * 1024 + j * 128, [[1, 128]]),
                                                                            identity=AP(ident, 0, [[1, 128]]))),
                               r=[("A", "qtok", tl, 0, "p"), ("A", "qtok", tl, 0, "r"), ("A", "qtok", tl, 1, "p"), ("A", "qtok", tl, 1, "r"), (CK, "ident")], w=[("ps", b)])
                    tt = tl - 1
                    pr.add("act", (lambda e, b=b, tt=tt: e.copy(out=AP(qT, tt * 128, [[512, 8], [1, 128]]), in_=AP(psb, b * 1024, [[128, 8], [1, 128]]))),
                           r=[("ps", b)], w=[("A", "qT", tt)])
                b = nb(abank_state["r"])
                for c in range(2):
                    pr.add("pe", (lambda e, c=c, b=b, tl=tl: e.transpose(out=AP(psb, b * 1024 + c * 128, [[1, 128]]), in_=AP(ktok, tl * 256 + c * 128, [[1, 128]]),
                                                                        identity=AP(ident, 0, [[1, 128]]))),
                           r=[("A", "ktok", tl, "p"), ("A", "ktok", tl, "r"), (CK, "ident")], w=[("ps", b)])
                pr.add("dve", (lambda e, b=b, tile=tile: e.tensor_copy(out=AP(KT, tile * 128, [[TOK, 2], [1, 128]]), in_=AP(psb, b * 1024, [[128, 2], [1, 128]]))),
                       r=[("ps", b)], w=[("KT", tile)])


        a_ops = capture(block_a)
        if pending_tail is None:
            pr.ops.extend(a_ops)
        else:
            g0keys = [("h1b", 0), ("h1b", 1), ("sg", 0), ("sg", 1)] + [("h1T", t_, h_) for t_ in range(4) for h_ in range(2)] + [("actT", f_) for f_ in range(22)]
            pr.add("act", lambda e: e.memzero(AP(fence, 0, [[1, 1]])), w=g0keys + ["fence_a"])
            pr.add("dve", lambda e: e.memset(AP(fence, 2, [[1, 1]]), 0.0), w=g0keys + ["fence_d"])
            n_ad = sum(1 for o_ in a_ops if o_.eng in ("act", "dve"))
            per = max(1, (n_ad // 2) // max(1, len(pending_tail)))
            ti_, k_ = 0, 0
            for ai_, o_ in enumerate(a_ops):
                if ai_ == a_mark["i"]:
                    pr.ops.extend(pending_tail[ti_:])
                    ti_ = len(pending_tail)
                pr.ops.append(o_)
                if o_.eng in ("act", "dve"):
                    k_ += 1
                    if k_ % per == 0 and ti_ < len(pending_tail):
                        pr.ops.append(pending_tail[ti_])
                        ti_ += 1
            pr.ops.extend(pending_tail[ti_:])
            pending_tail = None
        pr.barrier()

        BS = 7
        OB = (5, 6)
        rot = range(5)

        conv_banks = {}

        def cbu_mm(j):
            s = next_ws()
            for g in range(3):
                pr.add("pool", (lambda e, j=j, s=s, g=g: e.dma_start(out=AP(wsl[s], g * 128, [[384, 16], [1, 128]]),
                                                                    in_=DAP(w_in, 1536 + 1024 * g + 128 * j, [[INW, 128], [128 * INW, 16], [1, 128]]))),
                       w=[("w", s, g)], dma=("w", s, g))
            bc_, bb_, bu_ = nb(rot), nb(rot), nb(rot)
            bh = None
            for g, b in ((0, bc_), (2, bu_), (1, bb_)):
                for c in range(16):
                    pr.add("pe", (lambda e, c=c, b=b, s=s, g=g: e.matmul(PS(b), lhsT=AP(wsl[s], c * 384 + g * 128, [[1, 128]]), rhs=AP(xT, c * 640 + 128, [[1, 512]]),
                                                                        start=(c == 0), stop=(c == 15))),
                           r=[k for t_ in own for k in xTkeys(t_)] + [("w", s, g)], w=[("ps", b)])
            if p == 0:
                bh = nb(rot)
                for g, off in ((0, 0), (2, 2)):
                    for c in range(16):
                        pr.add("pe", (lambda e, c=c, bh=bh, s=s, g=g, off=off: e.matmul(PS(bh, off, 2), lhsT=AP(wsl[s], c * 384 + g * 128, [[1, 128]]), rhs=AP(xT, c * 640 + 126, [[1, 2]]),
                                                                                      start=(c == 0), stop=(c == 15))),
                               r=xTkeys(0) + [("w", s, g)], w=[("ps", bh)])
            conv_banks[j] = (bc_, bb_, bu_, bh)

        def conv_chain(j):
            bc_, bb_, bu_, bh = conv_banks[j]
            zs = j % 2
            zk = ("A", "zc", zs)
            if p == 0:
                o_uh = smalloc(2)
                pr.add("act", (lambda e, bh=bh, o_uh=o_uh: e.copy(out=AP(small, o_uh, [[1, 2]]), in_=PS(bh, 2, 2))), r=[("ps", bh)], w=[("uh", j)])
                pr.add("dve", (lambda e, bh=bh, o_uh=o_uh, zs=zs: e.tensor_tensor(out=AP(zc[zs], 0, [[1, 2]]), in0=PS(bh, 0, 2), in1=AP(small, o_uh, [[1, 2]]), op=ALU.mult)),
                       r=[("ps", bh), ("uh", j)], w=[zk])
            else:
                pr.add("dve", (lambda e, j=j, zs=zs: e.tensor_copy(out=AP(zc[zs], 0, [[1, 2]]), in_=AP(zh, 2 * j, [[1, 2]]))), r=[("zh", j)], w=[zk])
            pr.add("act", (lambda e, bu_=bu_, zs=zs: e.copy(out=AP(usb[zs], 0, [[1, 512]]), in_=PS(bu_))), r=[("ps", bu_)], w=[("A", "usb", zs)])
            pr.add("dve", (lambda e, bc_=bc_, zs=zs: e.tensor_tensor(out=AP(zc[zs], 2, [[1, 512]]), in0=PS(bc_), in1=AP(usb[zs], 0, [[1, 512]]), op=ALU.mult)),
                   r=[("ps", bc_), ("A", "usb", zs)], w=[zk])
            pr.add("act", (lambda e, j=j, zs=zs: e.copy(out=AP(zh, 2 * j, [[1, 2]]), in_=AP(zc[zs], 512, [[1, 2]]))), r=[zk], w=[("zh", j)])
            yk = ("A", "y", zs)
            pr.add("act", (lambda e, j=j, zs=zs: e.activation(out=AP(ybuf[zs], 0, [[1, 512]]), in_=AP(zc[zs], 2, [[1, 512]]), func=AF.Copy, scale=AP(convw, 3 * j + 2, [[1, 1]]))),
                   r=[zk, (CK, "convw")], w=[yk])
            pr.add("dve", (lambda e, j=j, zs=zs: e.scalar_tensor_tensor(out=AP(ybuf[zs], 0, [[1, 512]]), in0=AP(zc[zs], 1, [[1, 512]]), scalar=AP(convw, 3 * j + 1, [[1, 1]]),
                                                                      in1=AP(ybuf[zs], 0, [[1, 512]]), op0=ALU.mult, op1=ALU.add)),
                   r=[zk, yk, (CK, "convw")], w=[yk])
            pr.add("dve", (lambda e, j=j, zs=zs: e.scalar_tensor_tensor(out=AP(ybuf[zs], 0, [[1, 512]]), in0=AP(zc[zs], 0, [[1, 512]]), scalar=AP(convw, 3 * j, [[1, 1]]),
                                                                      in1=AP(ybuf[zs], 0, [[1, 512]]), op0=ALU.mult, op1=ALU.add)),
                   r=[zk, yk, (CK, "convw")], w=[yk])
            pr.add("dve", (lambda e, j=j, zs=zs, bb_=bb_: e.tensor_tensor(out=AP(obuf, j * 512, [[1, 512]]), in0=PS(bb_), in1=AP(ybuf[zs], 0, [[1, 512]]), op=ALU.mult)),
                   r=[("ps", bb_), yk], w=[("A", "o", j)])
            pr.add("act", (lambda e, j=j, zs=zs: e.activation(out=AP(sq[zs], 0, [[1, 512]]), in_=AP(obuf, j * 512, [[1, 512]]), func=AF.Square)),
                   r=[("A", "o", j)], w=[("A", "sq", zs)])

        def ssq_mm(j):
            zs = j % 2
            pr.add("pe", (lambda e, j=j, zs=zs: e.matmul(PS(BS), lhsT=AP(ones, 0, [[1, 128]]), rhs=AP(sq[zs], 0, [[1, 512]]), start=(j == 0), stop=(j == 7))),
                   r=[("A", "sq", zs), (CK, "ones")], w=[("ps", BS)])

        def conv_finish():
            pr.add("act", lambda e: e.activation(out=AP(rstd_c, 0, [[1, 512]]), in_=PS(BS), func=AF.Ln, scale=1.0 / 1024.0, bias=EPS_RMS),
                   r=[("ps", BS), (CK, "eps1")], w=[("A", "rstd_c")])
            pr.add("act", lambda e: e.activation(out=AP(rstd_c, 0, [[1, 512]]), in_=AP(rstd_c, 0, [[1, 512]]), func=AF.Exp, scale=-0.5),
                   r=[("A", "rstd_c")], w=[("A", "rstd_c")])
            for j in range(8):
                pr.add("dve", (lambda e, j=j: e.scalar_tensor_tensor(out=AP(mT, (8 + j) * 512, [[1, 512]]), in0=AP(obuf, j * 512, [[1, 512]]), scalar=AP(gconv, j, [[1, 1]]),
                                                                   in1=AP(rstd_c, 0, [[1, 512]]), op0=ALU.mult, op1=ALU.mult)),
                       r=[("A", "o", j), ("A", "rstd_c"), (CK, "gconv")], w=[("mT", 8 + j, tt) for tt in range(4)])

        att_state = {}

        def att_front(tt, half):
            g_prev = 4 * p + tt
            g_cur = g_prev + 1
            mprev = maskp0 if (p == 0 and tt == 0) else maskp
            info = []
            for kk in range(2):
                kv = 2 * half + kk
                p0 = 0 if kv < 2 else 64
                c = kv % 2
                ch0 = 4 * (kv % 2)
                pslot = (4 * tt + kv) % 3
                bs = [nb(rot), nb(rot)]
                for ci, tile in ((0, g_prev), (1, g_cur)):
                    pr.add("pe", (lambda e, b=bs[ci], c=c, tile=tile, p0=p0, ch0=ch0, tt=tt: e.matmul(
                        PS(b), lhsT=AP(KT, c * TOK + tile * 128, [[1, 128]], p0=p0, npart=64),
                        rhs=AP(qT, ch0 * 512 + tt * 128, [[512, 4], [1, 128]], p0=p0, npart=64), start=True, stop=True)),
                        r=[("KT", tile), ("A", "qT", tt)], w=[("ps", bs[ci])])
                for ci in range(2):
                    pr.add("act", (lambda e, b=bs[ci], ci=ci, pslot=pslot: e.activation(out=AP(pT[pslot], ci * 512, [[1, 512]]), in_=PS(b), func=AF.Exp, scale=ATTN_SCALE)),
                           r=[("ps", bs[ci])], w=[("A", "pT", pslot, ci)])
                    mk = mprev if ci == 0 else maskc
                    pr.add("dve", (lambda e, ci=ci, pslot=pslot, mk=mk: e.tensor_tensor(out=AP(pT[pslot], ci * 512, [[128, 4], [1, 128]]), in0=AP(pT[pslot], ci * 512, [[128, 4], [1, 128]]),
                                                                                      in1=AP(mk, 0, [[0, 4], [1, 128]]), op=ALU.mult)),
                           r=[("A", "pT", pslot, ci), (CK, cname[id(mk)])], w=[("A", "pT", pslot, ci)])
                info.append((kv, pslot))
            att_state[(tt, half)] = (info, g_prev, g_cur)

        def att_pv(tt, half):
            info, g_prev, g_cur = att_state[(tt, half)]
            for kk, (kv, pslot) in enumerate(info):
                for g4 in range(4):
                    for ci, tile in ((0, g_prev), (1, g_cur)):
                        pr.add("pe", (lambda e, g4=g4, ci=ci, tile=tile, kv=kv, pslot=pslot, b=OB[kk]: e.matmul(
                            PS(b, g4 * 65, 65), lhsT=AP(pT[pslot], ci * 512 + g4 * 128, [[1, 128]]), rhs=AP(V, tile * 260 + kv * 65, [[1, 65]]),
                            start=(ci == 0), stop=(ci == 1))),
                            r=[("A", "pT", pslot, ci), ("V", tile)], w=[("ps", OB[kk])])

        att_sm = {}

        def att_post(tt, half):
            asl = tt % 2
            if half == 0:
                att_sm[tt] = smalloc(2)
            o_ssq = att_sm[tt]
            o_den = smalloc(8)
            o_rden = smalloc(8)
            for kk in range(2):
                kv = 2 * half + kk
                pr.add("dve", (lambda e, b=OB[kk], kk=kk, kv=kv, o_den=o_den: e.tensor_tensor(out=AP(small, o_den + 4 * kk, [[1, 4]]), in0=AP(ps, b * 512 + 64, [[65, 4]]),
                                                                                           in1=AP(expsink, 4 * kv, [[1, 4]]), op=ALU.add)),
                       r=[("ps", OB[kk]), (CK, "sink")], w=[("A", "den", tt, half)])
            pr.add("dve", (lambda e, o_den=o_den, o_rden=o_rden: e.reciprocal(out=AP(small, o_rden, [[1, 8]]), in_=AP(small, o_den, [[1, 8]]))),
                   r=[("A", "den", tt, half)], w=[("A", "rden", tt, half)])
            for kk in range(2):
                kv = 2 * half + kk
                pr.add("dve", (lambda e, b=OB[kk], kk=kk, kv=kv, o_rden=o_rden, asl=asl: e.tensor_tensor(
                    out=AP(atok[asl], kv * 256, [[64, 4], [1, 64]]), in0=AP(ps, b * 512, [[65, 4], [1, 64]]),
                    in1=AP(small, o_rden + 4 * kk, [[1, 4], [0, 64]]), op=ALU.mult)),
                    r=[("ps", OB[kk]), ("A", "rden", tt, half)], w=[("A", "atok", asl, half)])
            pr.add("act", (lambda e, half=half, asl=asl, o_ssq=o_ssq: e.activation(out=AP(junk, 0, [[1, 512]]), in_=AP(atok[asl], half * 512, [[1, 512]]), func=AF.Square,
                                                                                 accum_out=AP(small, o_ssq + half, [[1, 1]]))),
                   r=[("A", "atok", asl, half)], w=[("A", "ssq", tt, half), ("A", "junk")])

        def att_final_a(tt):
            asl = tt % 2
            o_ssq = att_sm[tt]
            o_r = smalloc(2)
            pr.add("dve", (lambda e, o_ssq=o_ssq, o_r=o_r: e.tensor_tensor(out=AP(small, o_r, [[1, 1]]), in0=AP(small, o_ssq, [[1, 1]]), in1=AP(small, o_ssq + 1, [[1, 1]]), op=ALU.add)),
                   r=[("A", "ssq", tt, 0), ("A", "ssq", tt, 1)], w=[("A", "rs", tt)])
            pr.add("act", (lambda e, o_r=o_r: e.activation(out=AP(small, o_r, [[1, 1]]), in_=AP(small, o_r, [[1, 1]]), func=AF.Ln, scale=1.0 / 1024.0, bias=EPS_RMS)),
                   r=[("A", "rs", tt), (CK, "eps1")], w=[("A", "rs", tt)])
            pr.add("act", (lambda e, o_r=o_r: e.activation(out=AP(small, o_r + 1, [[1, 1]]), in_=AP(small, o_r, [[1, 1]]), func=AF.Exp, scale=-0.5)),
                   r=[("A", "rs", tt)], w=[("A", "rs2", tt)])
            pr.add("dve", (lambda e, o_r=o_r, asl=asl: e.scalar_tensor_tensor(out=AP(mtok[asl], 0, [[1, 1024]]), in0=AP(atok[asl], 0, [[1, 1024]]), scalar=AP(small, o_r + 1, [[1, 1]]),
                                                                            in1=AP(gattn, 0, [[1, 1024]]), op0=ALU.mult, op1=ALU.mult)),
                   r=[("A", "atok", asl, 0), ("A", "atok", asl, 1), ("A", "rs2", tt), (CK, "gattn")], w=[("A", "mtok", asl)])

        def att_final_b(tt):
            asl = tt % 2
            b = nb(rot)
            for j in range(8):
                pr.add("pe", (lambda e, j=j, b=b, asl=asl: e.transpose(out=AP(psb, b * 1024 + j * 128, [[1, 128]]), in_=AP(mtok[asl], j * 128, [[1, 128]]), identity=AP(ident, 0, [[1, 128]]))),
                       r=[("A", "mtok", asl), (CK, "ident")], w=[("ps", b)])
            pr.add("act", (lambda e, b=b, tt=tt: e.copy(out=AP(mT, tt * 128, [[512, 8], [1, 128]]), in_=AP(psb, b * 1024, [[128, 8], [1, 128]]))),
                   r=[("ps", b)], w=[("mT", j, tt) for j in range(8)])

        do_att = stop_after != "A4"
        for u in range(8):
            tt, half = u // 2, u % 2
            if do_att:
                att_front(tt, half)
            cbu_mm(u)
            if do_att:
                att_pv(tt, half)
            if u >= 1:
                ssq_mm(u - 1)
            if do_att and half == 0 and tt >= 1:
                att_final_b(tt - 1)
            conv_chain(u)
            if do_att:
                att_post(tt, half)
                if half == 1:
                    att_final_a(tt)
        ssq_mm(7)
        if do_att:
            att_final_b(3)
        conv_finish()

        if stop_after == "A4":
            break

        if stop_after == "B":
            break
        pr.barrier(exempt=("pe",))

        def ln_load(gd, bd):
            pr.add("sp", (lambda e, gd=gd: e.dma_start(out=AP(lngb, 0, [[1, 2048]]), in_=DAP(gd, 0, [[0, 128], [1, D]]))), w=[("lngb", 0)], dma=("lngb", 0))
            pr.add("sp", (lambda e, bd=bd: e.dma_start(out=AP(lngb, 2048, [[1, 2048]]), in_=DAP(bd, 0, [[0, 128], [1, D]]))), w=[("lngb", 1)], dma=("lngb", 1))

        def ln_stage_list(lsm, final):
            def stages(tt):
                hk = [("h1", tt, dt) for dt in range(4)]
                o = lsm[tt]
                S = lambda i: AP(small, o + i, [[1, 1]])
                k = lambda nm: ("lnk", nm, o)
                st = []
                st.append(lambda: pr.add("act", (lambda e: e.activation(out=AP(junk2, 0, [[1, 2048]]), in_=AP(h1, tt * 2048, [[1, 2048]]), func=AF.Square, accum_out=S(4))),
                                         r=hk, w=[k("sq"), "junk2"]))

                def tiny():
                    pr.add("dve", (lambda e: e.reduce_sum(out=S(5), in_=AP(small, o, [[1, 4]]), axis=mybir.AxisListType.X)), r=[("lns", tt, dt) for dt in range(4)], w=[k("tot")])
                    pr.add("dve", (lambda e: e.tensor_scalar(out=S(5), in0=S(5), scalar1=1.0 / D, scalar2=None, op0=ALU.mult)), r=[k("tot")], w=[k("mean")])
                    pr.add("dve", (lambda e: e.tensor_tensor(out=S(6), in0=S(5), in1=S(5), op=ALU.mult)), r=[k("mean")], w=[k("msq")])
                    pr.add("dve", (lambda e: e.scalar_tensor_tensor(out=S(6), in0=S(4), scalar=1.0 / D, in1=S(6), op0=ALU.mult, op1=ALU.subtract)), r=[k("sq"), k("msq")], w=[k("var")])
                    pr.add("act", (lambda e: e.activation(out=S(6), in_=S(6), func=AF.Ln, bias=EPS_LN)), r=[k("var"), (CK, "eps2")], w=[k("lnv")])
                    pr.add("act", (lambda e: e.activation(out=S(7), in_=S(6), func=AF.Exp, scale=-0.5)), r=[k("lnv")], w=[k("rstd")])
                    pr.add("dve", (lambda e: e.tensor_scalar(out=S(8), in0=S(5), scalar1=S(7), scalar2=-1.0, op0=ALU.mult, op1=ALU.mult)), r=[k("mean"), k("rstd")], w=[k("nmr")])
                st.append(tiny)
                def affine():
                    hb = lambda h: [("ps", 2 * h), ("ps", 2 * h + 1)]
                    hkh = lambda h: [("h1", tt, 2 * h), ("h1", tt, 2 * h + 1)]
                    for h in range(2):
                        pr.add("act", (lambda e, h=h: e.activation(out=AP(ps, h * 1024, [[1, 1024]]), in_=AP(h1, tt * 2048 + h * 1024, [[1, 1024]]), func=AF.Identity, scale=S(7), bias=S(8))),
                               r=hkh(h) + [k("rstd"), k("nmr")], w=hb(h))
                    for h in range(2):
                        pr.add("dve", (lambda e, h=h: e.tensor_tensor(out=AP(ps, h * 1024, [[1, 1024]]), in0=AP(ps, h * 1024, [[1, 1024]]), in1=AP(lngb, h * 1024, [[1, 1024]]), op=ALU.mult)),
                               r=[("lngb", 0)], w=hb(h))
                        pr.add("dve", (lambda e, h=h: e.tensor_tensor(out=AP(h1, tt * 2048 + h * 1024, [[1, 1024]]), in0=AP(ps, h * 1024, [[1, 1024]]), in1=AP(lngb, 2048 + h * 1024, [[1, 1024]]), op=ALU.add)),
                               r=[("lngb", 1)] + hb(h), w=hkh(h))
                st.append(affine)
                if final:
                    r0 = (4 * p + tt) * 128
                    st.append(lambda: pr.add("sp", (lambda e: e.dma_start(out=out_c[r0:r0 + 128, :], in_=AP(h1, tt * 2048, [[1, 2048]]))), r=hk, dma=("h1", tt)))
                else:
                    hs = tt % 2
                    b0, b1 = (4, 5) if tt % 2 == 0 else (6, 7)
                    st.append(lambda: pr.add("act", (lambda e: e.copy(out=AP(h1b[hs], 0, [[1, 2048]]), in_=AP(h1, tt * 2048, [[1, 2048]]))), r=hk, w=[("h1b", hs)]))

                    def tr():
                        for c in range(16):
                            b = b0 if c < 8 else b1
                            pr.add("pe", (lambda e, c=c, b=b: e.transpose(out=AP(psb, b * 1024 + (c % 8) * 128, [[1, 128]]), in_=AP(h1b[hs], c * 128, [[1, 128]]), identity=AP(ident, 0, [[1, 128]]))),
                                   r=[("h1b", hs), (CK, "ident")], w=[("ps", b)])
                    st.append(tr)

                    def ev():
                        pr.add("act", (lambda e: e.copy(out=AP(h1T, tt * 128, [[512, 8], [1, 128]]), in_=AP(psb, b0 * 1024, [[128, 8], [1, 128]]))), r=[("ps", b0)], w=[("h1T", tt, 0)])
                        pr.add("dve", (lambda e: e.tensor_copy(out=AP(h1T, 8 * 512 + tt * 128, [[512, 8], [1, 128]]), in_=AP(psb, b1 * 1024, [[128, 8], [1, 128]]))), r=[("ps", b1)], w=[("h1T", tt, 1)])
                    st.append(ev)
                return st

            return stages

        ln_load(ln1g_d, ln1b_d)
        ln_sm = [smalloc(12) for _ in range(4)]
        ln1_stages = ln_stage_list(ln_sm, final=False)
        ln1_st = [None] * 4
        for tt in range(4):
            r0 = (1 + 4 * p + tt) * 128
            pr.add("sp", (lambda e, tt=tt, r0=r0: e.dma_start(out=AP(h1, tt * 2048, [[1, 2048]]), in_=x_c[r0:r0 + 128, :])),
                   w=[("h1", tt, dt) for dt in range(4)], dma=("h1", tt))
        for dt in range(4):
            s = next_ws()
            pr.add("pool", (lambda e, dt=dt, s=s: e.dma_start(out=AP(wsl[s], 0, [[512, 16], [1, 512]]), in_=DAP(w_out, dt * 512, [[D, 128], [128 * D, 16], [1, 512]]))),
                   w=WK(s), dma=("w", s, 0))
            for tt in range(4):
                b = nb()
                for c in range(16):
                    pr.add("pe", (lambda e, c=c, b=b, s=s, tt=tt: e.matmul(PS(b), lhsT=AP(mT, c * 512 + tt * 128, [[1, 128]]), rhs=AP(wsl[s], c * 512, [[1, 512]]),
                                                                          start=(c == 0), stop=(c == 15))),
                           r=[("mT", c, tt)] + WK(s), w=[("ps", b)])
                pr.add("dve", (lambda e, b=b, tt=tt, dt=dt, oacc=ln_sm[tt] + dt: e.scalar_tensor_tensor(out=AP(h1, tt * 2048 + dt * 512, [[1, 512]]), in0=AP(h1, tt * 2048 + dt * 512, [[1, 512]]), scalar=ALPHA,
                                                                                  in1=PS(b), op0=ALU.mult, op1=ALU.add, accum_out=AP(small, oacc, [[1, 1]]))),
                       r=[("ps", b), ("h1", tt, dt)], w=[("h1", tt, dt), ("lns", tt, dt)])

        ln1_st = [ln1_stages(tt) for tt in range(4)]
        for wv in range(len(ln1_st[0]) + 3):
            for tt in range(4):
                si = wv - tt
                if 0 <= si < len(ln1_st[0]):
                    ln1_st[tt][si]()

        if stop_after == "C":
            break

        ln_load(ln2g_d, ln2b_d)
        ln_sm2 = [smalloc(12) for _ in range(4)]
        ln2_stages = ln_stage_list(ln_sm2, final=True)
        h1Tk = [("h1T", tt, hh) for tt in range(4) for hh in range(2)]
        for half in range(2):
            for fp in range(11):
                s = next_ws()
                col0 = (half * 22 + 2 * fp) * 128
                pr.add("pool", (lambda e, s=s, col0=col0: e.dma_start(out=AP(wsl[s], 0, [[512, 16], [1, 256]]), in_=DAP(w_gate, col0, [[DFF, 128], [128 * DFF, 16], [1, 256]]))),
                       w=[("w", s, 0)], dma=("w", s, 0))
                pr.add("pool", (lambda e, s=s, col0=col0: e.dma_start(out=AP(wsl[s], 256, [[512, 16], [1, 256]]), in_=DAP(w_up, col0, [[DFF, 128], [128 * DFF, 16], [1, 256]]))),
                       w=[("w", s, 1)], dma=("w", s, 1))
                for i in range(2):
                    fcl = 2 * fp + i
                    bg, bu = nb(), nb()
                    for gi, b in ((0, bg), (1, bu)):
                        for c in range(16):
                            pr.add("pe", (lambda e, c=c, b=b, s=s, gi=gi, i=i: e.matmul(PS(b), lhsT=AP(wsl[s], c * 512 + gi * 256 + i * 128, [[1, 128]]), rhs=AP(h1T, c * 512, [[1, 512]]),
                                                                                      start=(c == 0), stop=(c == 15))),
                                   r=h1Tk + [("w", s, gi)], w=[("ps", b)])
                    ss = fcl % 2
                    pr.add("act", (lambda e, bg=bg, ss=ss: e.activation(out=AP(sg[ss], 0, [[1, 512]]), in_=PS(bg), func=AF.Silu)), r=[("ps", bg)], w=[("sg", ss)])
                    pr.add("dve", (lambda e, bu=bu, ss=ss, fcl=fcl: e.tensor_tensor(out=AP(actT, fcl * 512, [[1, 512]]), in0=PS(bu), in1=AP(sg[ss], 0, [[1, 512]]), op=ALU.mult)),
                           r=[("ps", bu), ("sg", ss)], w=[("actT", fcl)])
            for dt in range(4):
                last = (half == 1 and dt == 3)
                bk = [nb(range(4, 8)) for _ in range(4)] if last else [nb() for _ in range(4)]
                for piece in range(2):
                    s = next_ws()
                    f0 = (half * 22 + piece * 11) * 128
                    pr.add("pool", (lambda e, s=s, f0=f0, dt=dt: e.dma_start(out=AP(wsl[s], 0, [[512, 11], [1, 512]]), in_=DAP(w_down, f0 * D + dt * 512, [[D, 128], [128 * D, 11], [1, 512]]))),
                           w=WK(s), dma=("w", s, 0))
                    for tt in range(4):
                        for k in range(11):
                            fcl = piece * 11 + k
                            pr.add("pe", (lambda e, k=k, fcl=fcl, tt=tt, s=s, b=bk[tt], piece=piece: e.matmul(PS(b), lhsT=AP(actT, fcl * 512 + tt * 128, [[1, 128]]), rhs=AP(wsl[s], k * 512, [[1, 512]]),
                                                                                                          start=(piece == 0 and k == 0), stop=(piece == 1 and k == 10))),
                                   r=[("actT", fcl)] + WK(s), w=[("ps", bk[tt])])
                for tt in range(4):
                    if half == 0:
                        pr.add("dve", (lambda e, b=bk[tt], tt=tt, dt=dt: e.scalar_tensor_tensor(out=AP(h1, tt * 2048 + dt * 512, [[1, 512]]), in0=AP(h1, tt * 2048 + dt * 512, [[1, 512]]), scalar=ALPHA,
                                                                                              in1=PS(b), op0=ALU.mult, op1=ALU.add)),
                               r=[("ps", bk[tt]), ("h1", tt, dt)], w=[("h1", tt, dt)])
                    else:
                        pr.add("dve", (lambda e, b=bk[tt], tt=tt, dt=dt, oacc=ln_sm2[tt] + dt: e.scalar_tensor_tensor(out=AP(h1, tt * 2048 + dt * 512, [[1, 512]]), in0=AP(h1, tt * 2048 + dt * 512, [[1, 512]]), scalar=1.0,
                                                                                              in1=PS(b), op0=ALU.mult, op1=ALU.add, accum_out=AP(small, oacc, [[1, 1]]))),
                               r=[("ps", bk[tt]), ("h1", tt, dt)], w=[("h1", tt, dt), ("lns", tt, dt)])
        tail_start = len(pr.ops)
        ln2_st = [ln2_stages(tt) for tt in range(4)]
        for wv in range(len(ln2_st[0]) + 3):
            for tt in range(4):
                si = wv - tt
                if 0 <= si < len(ln2_st[0]):
                    ln2_st[tt][si]()
        pending_tail = pr.ops[tail_start:]
        del pr.ops[tail_start:]

    if pending_tail is not None:
        pr.ops.extend(pending_tail)
        pending_tail = None
    if debug:
        pr.barrier()
        for name, (t, dt) in dumps.items():
            dd = nc.dram_tensor("dbg_" + name, [128, t.shape[1]], dt, kind="ExternalOutput")
            pr.add("sp", (lambda e, t=t, dd=dd: e.dma_start(out=dd.ap(), in_=AP(t, 0, [[1, t.shape[1]]]))), dma=("dbg", name))
    pr.emit()
    return nc, pr


def _host_consts():
    bf = ml_dtypes.bfloat16
    ident = np.eye(128, dtype=np.float32).astype(bf)
    k = np.arange(128)[:, None]
    q = np.arange(128)[None, :]
    maskp = (k > q).astype(np.float32).astype(bf)
    maskc = (k <= q).astype(np.float32).astype(bf)
    inv64 = 500000.0 ** (-np.arange(0, 16, 2, dtype=np.float64) / 16.0)
    hi = inv64.astype(np.float32)
    lo = (inv64 - hi.astype(np.float64)).astype(np.float32)
    invf = np.broadcast_to(np.concatenate([hi, lo])[None, :], (128, 16)).copy()
    return ident, maskp, maskc, invf


_CACHE = {}


def make_in_maps(x, positions, w_in, conv_w, sinks, g_attn, g_conv, w_out, ln1_g, ln1_b, w_gate, w_up, w_down, ln2_g, ln2_b, cores=range(8)):
    ident, maskp, maskc, invf = _host_consts()
    f = lambda a: np.ascontiguousarray(np.asarray(a, dtype=np.float32))
    x = np.asarray(x)
    positions = np.asarray(positions)
    shared = {
        "w_in": f(w_in[0]), "w_out": f(w_out[0]), "w_gate": f(w_gate[0]), "w_up": f(w_up[0]), "w_down": f(w_down[0]),
        "convw": np.ascontiguousarray(np.asarray(conv_w[0], np.float32).reshape(3, 8, 128).transpose(2, 1, 0).reshape(128, 24)),
        "gconv": np.ascontiguousarray(np.asarray(g_conv[0], np.float32).reshape(8, 128).T),
        "gattn": f(g_attn[0]).reshape(1, 1024), "sinks": f(sinks[0]).reshape(1, 16),
        "ln1g": f(ln1_g[0]).reshape(1, D), "ln1b": f(ln1_b[0]).reshape(1, D), "ln2g": f(ln2_g[0]).reshape(1, D), "ln2b": f(ln2_b[0]).reshape(1, D),
        "ident": ident, "maskp": maskp, "maskc": maskc, "invf": invf,
    }
    maps = []
    for core in cores:
        b, half = core // 2, core % 2
        s0 = half * 2048
        xc = np.zeros((TOK, D), np.float32)
        pc = np.zeros((TOK,), np.int32)
        xc[128:] = x[b, s0:s0 + 2048]
        pc[128:] = positions[b, s0:s0 + 2048]
        if half == 1:
            xc[:128] = x[b, s0 - 128:s0]
            pc[:128] = positions[b, s0 - 128:s0]
        m = dict(shared)
        m["x_c"] = xc
        m["pos_c"] = np.ascontiguousarray(pc.reshape(NTILE, 128).T)
        m["maskp0"] = maskp if half == 1 else np.zeros_like(maskp)
        maps.append(m)
    return maps


def kernel(**inputs):
    if "nc" not in _CACHE:
        _CACHE["nc"] = build()[0]
    nc = _CACHE["nc"]
    maps = make_in_maps(**inputs)
    res = run_bass_kernel_spmd(nc, maps, core_ids=list(range(8)))
    out = np.empty((4, 4096, D), np.float32)
    for core in range(8):
        b, half = core // 2, core % 2
        out[b, half * 2048:(half + 1) * 2048] = np.asarray(res.results[core]["out_c"])
    return out
```
